# Optimizing a Trainium2 kernel written in Bass

```python
import math
import jax, jax.numpy as jnp
from jax import lax
import numpy as np

D_MODEL = 2048
BATCH = 4
SEQ = 2048
DEPTH = 1
DEC_BATCH = 128
DEC_SEQ = 1
PAST_LEN = 16384
PAGE_SIZE = 128

N_META = 16
SSM_WIDTH = D_MODEL // 2
SSM_GROUP = 16
SSM_GROUPS = SSM_WIDTH // SSM_GROUP
SSM_STATE = 64
MLSTM_WIDTH = D_MODEL // 2
MLSTM_HEADS = 4
MLSTM_DK = MLSTM_WIDTH // MLSTM_HEADS
MLSTM_DV = MLSTM_WIDTH // MLSTM_HEADS
CHUNK = 128
D_FF = -(-8 * D_MODEL // (3 * 256)) * 256
EPS = 1e-5
ALPHA = (2 * DEPTH) ** 0.25
BETA = (8 * DEPTH) ** -0.25

N_IN = SSM_WIDTH + 2 * MLSTM_HEADS * MLSTM_DK + MLSTM_HEADS * MLSTM_DV + MLSTM_WIDTH + 2 * MLSTM_HEADS + 2 * D_MODEL
_SPLIT_SIZES = (SSM_WIDTH, MLSTM_HEADS * MLSTM_DK, MLSTM_HEADS * MLSTM_DK, MLSTM_HEADS * MLSTM_DV,
                MLSTM_WIDTH, 2 * MLSTM_HEADS, D_MODEL)
SPLITS = tuple(int(s) for s in np.cumsum(_SPLIT_SIZES))

kernel_name = "hybrid_s5_mlstm_gated_decoder_step"


def _layernorm(x, g, b):
    xf = x.astype(jnp.float32)
    mu = xf.mean(-1, keepdims=True)
    var = jnp.mean(jnp.square(xf - mu), -1, keepdims=True)
    return ((xf - mu) * lax.rsqrt(var + EPS) * g + b).astype(x.dtype)


def _ssm_combine(e1, e2):
    a1, b1 = e1
    a2, b2 = e2
    return a1 * a2, a2 * b1 + b2


def _s5_branch(u, h_re, h_im, a_re, a_im, log_dt, b_re, b_im, c_re, c_im, d_skip, w_glu, b_glu):
    bt, t, _ = u.shape
    f32 = jnp.float32
    lam = lax.complex(a_re.astype(f32), a_im.astype(f32))
    dt = jnp.exp(log_dt.astype(f32))
    a_bar = jnp.exp(lam * dt)
    b_bar = ((a_bar - 1.0) / lam)[..., None] * lax.complex(b_re.astype(f32), b_im.astype(f32))
    ug = u.reshape(bt, t, SSM_GROUPS, SSM_GROUP)
    bu = jnp.einsum("btgc,gpc->btgp", ug.astype(jnp.complex64), b_bar)
    h0 = lax.complex(h_re.astype(f32), h_im.astype(f32))
    bu = bu.at[:, 0].add(a_bar * h0)
    a_seq = jnp.broadcast_to(a_bar, bu.shape)
    _, hs = lax.associative_scan(_ssm_combine, (a_seq, bu), axis=1)
    c = lax.complex(c_re.astype(f32), c_im.astype(f32))
    y = jnp.einsum("btgp,gcp->btgc", hs, c).real + d_skip.astype(f32).reshape(SSM_GROUPS, SSM_GROUP) * ug
    y = y.reshape(bt, t, SSM_WIDTH)
    g = jax.nn.gelu(y)
    out = g * jax.nn.sigmoid(g @ w_glu.astype(f32) + b_glu.astype(f32))
    h_last = hs[:, -1]
    return out, h_last.real, h_last.imag


def _mlstm_chunk(state, q, k, v, ig, fl):
    c, n, m = state
    length = q.shape[2]
    b = jnp.cumsum(fl, axis=-1)
    causal = jnp.tril(jnp.ones((length, length), dtype=bool))
    dmat = jnp.where(causal, b[..., :, None] - b[..., None, :] + ig[..., None, :], -jnp.inf)
    inter = b + m[..., None]
    m_t = jnp.maximum(inter, dmat.max(-1))
    w_intra = jnp.exp(dmat - m_t[..., None])
    w_inter = jnp.exp(inter - m_t)
    s = jnp.einsum("bhtk,bhsk->bhts", q, k) * w_intra
    num = w_inter[..., None] * jnp.einsum("bhtk,bhkv->bhtv", q, c) + jnp.einsum("bhts,bhsv->bhtv", s, v)
    den = w_inter * jnp.einsum("bhtk,bhk->bht", q, n) + s.sum(-1)
    h = num / jnp.maximum(jnp.abs(den), jnp.exp(-m_t))[..., None]
    m_new = m_t[..., -1]
    w_end = jnp.exp(b[..., -1:] - b + ig - m_new[..., None])
    carry = jnp.exp(b[..., -1] + m - m_new)
    c_new = carry[..., None, None] * c + jnp.einsum("bhs,bhsk,bhsv->bhkv", w_end, k, v)
    n_new = carry[..., None] * n + jnp.einsum("bhs,bhsk->bhk", w_end, k)
    return (c_new, n_new, m_new), h


def _mlstm_branch(q, k, v, ig, fl, state, lead):
    state, h_lead = _mlstm_chunk(state, q[:, :, :lead], k[:, :, :lead], v[:, :, :lead],
                                 ig[:, :, :lead], fl[:, :, :lead])
    t = q.shape[2]
    if t == lead:
        return h_lead, state
    nc = (t - lead) // CHUNK

    def blocks(z):
        z = z[:, :, lead:]
        return jnp.moveaxis(z.reshape(z.shape[:2] + (nc, CHUNK) + z.shape[3:]), 2, 0)

    def step(carry, xs):
        return _mlstm_chunk(carry, *xs)

    state, h_rest = lax.scan(step, state, (blocks(q), blocks(k), blocks(v), blocks(ig), blocks(fl)))
    bsz, nh = q.shape[0], q.shape[1]
    h_rest = jnp.moveaxis(h_rest, 0, 2).reshape(bsz, nh, nc * CHUNK, MLSTM_DV)
    return jnp.concatenate([h_lead, h_rest], axis=2), state


def _layer(x, h_re, h_im, c0, n0, m0, lead, w_in, b_if, a_re, a_im, log_dt, b_re, b_im, c_re, c_im,
           d_skip, w_glu, b_glu, w_a_up, mh_gain, w_b_up, w_out, ln1_g, ln1_b, w_gate, w_up, w_down,
           ln2_g, ln2_b):
    f32 = jnp.float32
    bt, t, _ = x.shape
    proj = (x @ w_in).astype(f32)
    u, q, k, v, o, g_if, g_a, g_b = jnp.split(proj, SPLITS, axis=-1)
    y_a, h_re_new, h_im_new = _s5_branch(u, h_re, h_im, a_re, a_im, log_dt, b_re, b_im, c_re, c_im,
                                         d_skip, w_glu, b_glu)
    def heads(z):
        return z.reshape(bt, t, MLSTM_HEADS, -1).transpose(0, 2, 1, 3)
    g_if = g_if + b_if.astype(f32)
    ig = g_if[..., :MLSTM_HEADS].transpose(0, 2, 1)
    fl = jax.nn.log_sigmoid(g_if[..., MLSTM_HEADS:]).transpose(0, 2, 1)
    state0 = (c0.astype(f32), n0.astype(f32), m0.astype(f32))
    h, (c_new, n_new, m_new) = _mlstm_branch(heads(q), heads(k) * (MLSTM_DK ** -0.5), heads(v), ig, fl,
                                             state0, lead)
    mu = h.mean(-1, keepdims=True)
    var = jnp.mean(jnp.square(h - mu), -1, keepdims=True)
    hn = ((h - mu) * lax.rsqrt(var + EPS)).transpose(0, 2, 1, 3).reshape(bt, t, MLSTM_WIDTH)
    y_b = jax.nn.sigmoid(o) * (hn * mh_gain.astype(f32))
    mix = jax.nn.sigmoid(g_a) * (y_a @ w_a_up.astype(f32)) + jax.nn.sigmoid(g_b) * (y_b @ w_b_up.astype(f32))
    mix = (mix @ w_out.astype(f32)).astype(x.dtype)
    x = _layernorm(ALPHA * x + mix, ln1_g, ln1_b)
    ff = (jax.nn.silu(x @ w_gate) * (x @ w_up)) @ w_down
    x = _layernorm(ALPHA * x + ff.astype(x.dtype), ln2_g, ln2_b)
    return x, (h_re_new, h_im_new, c_new, n_new, m_new)


def setup_inputs(seed: int = 0) -> dict:
    key = jax.random.key(seed)
    ks = jax.random.split(key, 32)
    f32 = jnp.float32

    def nrm(k, shape, s):
        return s * jax.random.normal(k, shape, f32)

    G, P, H = SSM_GROUPS, SSM_STATE, MLSTM_HEADS
    b_if = jnp.concatenate([
        nrm(ks[10], (DEPTH, H), 0.1),
        jnp.broadcast_to(jnp.linspace(3.0, 6.0, H, dtype=f32), (DEPTH, H)) + nrm(ks[11], (DEPTH, H), 0.1)],
        axis=-1)
    return {
        "x_prompt": nrm(ks[0], (BATCH, SEQ, D_MODEL), 1.0),
        "x_sample": nrm(ks[1], (DEC_BATCH, DEC_SEQ, D_MODEL), 1.0),
        "state_ssm_re": nrm(ks[2], (DEPTH, DEC_BATCH, G, P), 1.0),
        "state_ssm_im": nrm(ks[3], (DEPTH, DEC_BATCH, G, P), 1.0),
        "state_mlstm_c": nrm(ks[4], (DEPTH, DEC_BATCH, H, MLSTM_DK, MLSTM_DV), 0.1),
        "state_mlstm_n": nrm(ks[5], (DEPTH, DEC_BATCH, H, MLSTM_DK), 0.1),
        "state_mlstm_m": nrm(ks[6], (DEPTH, DEC_BATCH, H), 1.0),
        "meta_tokens": nrm(ks[7], (N_META, D_MODEL), 1.0),
        "w_in": nrm(ks[8], (DEPTH, D_MODEL, N_IN), D_MODEL ** -0.5),
        "b_if": b_if,
        "ssm_a_re": -0.5 + nrm(ks[12], (DEPTH, G, P), 0.01),
        "ssm_a_im": jnp.pi * jnp.arange(P, dtype=f32) + nrm(ks[13], (DEPTH, G, P), 0.01),
        "ssm_log_dt": jax.random.uniform(ks[14], (DEPTH, G, P), f32, math.log(1e-3), math.log(1e-1)),
        "ssm_b_re": nrm(ks[15], (DEPTH, G, P, SSM_GROUP), (2 * SSM_GROUP) ** -0.5),
        "ssm_b_im": nrm(ks[16], (DEPTH, G, P, SSM_GROUP), (2 * SSM_GROUP) ** -0.5),
        "ssm_c_re": nrm(ks[17], (DEPTH, G, SSM_GROUP, P), (2 * P) ** -0.5),
        "ssm_c_im": nrm(ks[18], (DEPTH, G, SSM_GROUP, P), (2 * P) ** -0.5),
        "ssm_d": nrm(ks[19], (DEPTH, SSM_WIDTH), 1.0),
        "w_glu": nrm(ks[20], (DEPTH, SSM_WIDTH, SSM_WIDTH), SSM_WIDTH ** -0.5),
        "b_glu": nrm(ks[21], (DEPTH, SSM_WIDTH), 0.01),
        "w_a_up": nrm(ks[22], (DEPTH, SSM_WIDTH, D_MODEL), SSM_WIDTH ** -0.5),
        "mh_gain": 1.0 + nrm(ks[23], (DEPTH, MLSTM_WIDTH), 0.01),
        "w_b_up": nrm(ks[24], (DEPTH, MLSTM_WIDTH, D_MODEL), MLSTM_WIDTH ** -0.5),
        "w_out": nrm(ks[25], (DEPTH, D_MODEL, D_MODEL), BETA * D_MODEL ** -0.5),
        "ln1_g": 1.0 + nrm(ks[26], (DEPTH, D_MODEL), 0.01),
        "ln1_b": nrm(ks[27], (DEPTH, D_MODEL), 0.01),
        "w_gate": nrm(ks[28], (DEPTH, D_MODEL, D_FF), D_MODEL ** -0.5),
        "w_up": nrm(ks[29], (DEPTH, D_MODEL, D_FF), D_MODEL ** -0.5),
        "w_down": nrm(ks[30], (DEPTH, D_FF, D_MODEL), BETA * D_FF ** -0.5),
        "ln2_g": 1.0 + nrm(ks[31], (DEPTH, D_MODEL), 0.01),
        "ln2_b": nrm(ks[9], (DEPTH, D_MODEL), 0.01),
    }


def reference(x_prompt, x_sample, state_ssm_re, state_ssm_im, state_mlstm_c, state_mlstm_n, state_mlstm_m,
              meta_tokens, w_in, b_if, ssm_a_re, ssm_a_im, ssm_log_dt, ssm_b_re, ssm_b_im, ssm_c_re, ssm_c_im,
              ssm_d, w_glu, b_glu, w_a_up, mh_gain, w_b_up, w_out, ln1_g, ln1_b, w_gate, w_up, w_down,
              ln2_g, ln2_b):
    f32 = jnp.float32
    bp = x_prompt.shape[0]
    meta = jnp.broadcast_to(meta_tokens.astype(x_prompt.dtype)[None], (bp, N_META, D_MODEL))
    xp = jnp.concatenate([meta, x_prompt], axis=1)
    xs = x_sample
    new_p = ([], [], [], [], [])
    new_s = ([], [], [], [], [])
    for l in range(DEPTH):
        lw = (w_in[l], b_if[l], ssm_a_re[l], ssm_a_im[l], ssm_log_dt[l], ssm_b_re[l], ssm_b_im[l],
              ssm_c_re[l], ssm_c_im[l], ssm_d[l], w_glu[l], b_glu[l], w_a_up[l], mh_gain[l], w_b_up[l],
              w_out[l], ln1_g[l], ln1_b[l], w_gate[l], w_up[l], w_down[l], ln2_g[l], ln2_b[l])
        z_ssm = jnp.zeros((bp, SSM_GROUPS, SSM_STATE), f32)
        z_c = jnp.zeros((bp, MLSTM_HEADS, MLSTM_DK, MLSTM_DV), f32)
        z_n = jnp.zeros((bp, MLSTM_HEADS, MLSTM_DK), f32)
        z_m = jnp.zeros((bp, MLSTM_HEADS), f32)
        xp, st_p = _layer(xp, z_ssm, z_ssm, z_c, z_n, z_m, N_META, *lw)
        xs, st_s = _layer(xs, state_ssm_re[l], state_ssm_im[l], state_mlstm_c[l], state_mlstm_n[l],
                          state_mlstm_m[l], xs.shape[1], *lw)
        for lst, a in zip(new_p, st_p):
            lst.append(a)
        for lst, a in zip(new_s, st_s):
            lst.append(a)
    y_prompt = xp[:, N_META:]
    y_sample = xs
    p_ssm_re = jnp.stack(new_p[0])
    p_ssm_im = jnp.stack(new_p[1])
    p_mlstm_c = jnp.stack(new_p[2])
    p_mlstm_n = jnp.stack(new_p[3])
    p_mlstm_m = jnp.stack(new_p[4])
    s_ssm_re = jnp.stack(new_s[0])
    s_ssm_im = jnp.stack(new_s[1])
    s_mlstm_c = jnp.stack(new_s[2])
    s_mlstm_n = jnp.stack(new_s[3])
    s_mlstm_m = jnp.stack(new_s[4])
    return (y_prompt, y_sample, p_ssm_re, p_ssm_im, p_mlstm_c, p_mlstm_n, p_mlstm_m,
            s_ssm_re, s_ssm_im, s_mlstm_c, s_mlstm_n, s_mlstm_m)
```

```python
import math
from contextlib import ExitStack
import numpy as np
import concourse.bass as bass
import concourse.mybir as mybir
from concourse.bass_utils import run_bass_kernel_spmd

F32 = mybir.dt.float32
BF16 = mybir.dt.bfloat16
I32 = mybir.dt.int32
AF = mybir.ActivationFunctionType
ALU = mybir.AluOpType

D = 2048
SEQ = 2048
NMETA = 16
NSAMP = 16
G, PST, SG = 64, 64, 16
H, DK, DV = 4, 256, 256
DFF = 5632
NIN = 9224
EPS = 1e-5
ALPHA = 2.0 ** 0.25
NCH = 2
NMAIN = 1024
NPRE = 1040
NEG = -30000.0
C_U, C_Q, C_K, C_V, C_O, C_IF, C_GA, C_GB = 0, 1024, 2048, 3072, 4096, 5120, 5128, 7176


class Buf:
    def __init__(self, t, name):
        self.t = t
        self.name = name
        self.acc = []
        self.dsem = None
        self.dcnt = 0

    def __getitem__(self, idx):
        return self.t[idx]

    @staticmethod
    def _conf(p, q):
        return p is None or q is None or p == q

    def deps(self, part, kind):
        out = []
        for (p, k, r) in self.acc:
            if not self._conf(p, part):
                continue
            if kind == 'r' and k == 'r':
                continue
            out.append(r)
        return out

    def record(self, part, kind, ref):
        if kind == 'w':
            self.acc = [(p, k, r) for (p, k, r) in self.acc if not self._conf(p, part)]
        else:
            self.acc = [(p, k, r) for (p, k, r) in self.acc
                        if not (k == 'r' and p == part and r[0] == ref[0])]
        self.acc.append((part, kind, ref))


class Prog:
    ENG = ('pe', 'act', 'dve', 'pool', 'sp')

    def __init__(self, nc, stack):
        self.nc = nc
        self.stack = stack
        self.semstack = stack
        self.eng = {'pe': nc.tensor, 'act': nc.scalar, 'dve': nc.vector, 'pool': nc.gpsimd, 'sp': nc.sync}
        self.sem = {}
        self.cnt = {}
        for e in self.ENG:
            self.sem[e] = stack.enter_context(nc.semaphore('s_' + e))
            self.cnt[e] = 0
        self.wm = {}
        self.dma_bufs = []
        self.n_inst = 0
        self.uid = 0

    def sbuf(self, name, shape, dt):
        return Buf(self.stack.enter_context(self.nc.sbuf_tensor(name, list(shape), dt)), name)

    def psum(self, name, shape, dt=F32):
        return Buf(self.stack.enter_context(self.nc.psum_tensor(name, list(shape), dt)), name)

    def dram(self, name, shape, dt, kind):
        return Buf(self.nc.dram_tensor(name, list(shape), dt, kind=kind), name)

    def _wait(self, e, ref):
        key, sem, val = ref[0], ref[2], ref[3]
        k = (e, key)
        if self.wm.get(k, 0) >= val:
            return
        self.wm[k] = val
        self.eng[e].wait_ge(sem, val)
        self.n_inst += 1

    @staticmethod
    def _norm(lst):
        return [(x, None) if isinstance(x, Buf) else x for x in lst]

    def _gather(self, reads, writes):
        deps = []
        for b, p in reads:
            deps += b.deps(p, 'r')
        for b, p in writes:
            deps += b.deps(p, 'w')
        return deps

    def op(self, e, fn, reads=(), writes=()):
        reads, writes = self._norm(reads), self._norm(writes)
        for r in self._gather(reads, writes):
            if e == 'pe' and r[1] == 'pe' and r[0][0] == 'c':
                continue
            self._wait(e, r)
        inst = fn(self.eng[e])
        self.cnt[e] += 1
        inst.then_inc(self.sem[e], 1)
        self.n_inst += 1
        ref = ('c' + e, e, self.sem[e], self.cnt[e])
        for b, p in reads:
            b.record(p, 'r', ref)
        for b, p in writes:
            b.record(p, 'w', ref)
        return inst

    def dma(self, q, out_ap, in_ap, reads=(), writes=(), owner=None, **kw):
        reads, writes = self._norm(reads), self._norm(writes)
        for r in self._gather(reads, writes):
            self._wait(q, r)
        part = None
        for b, p in list(writes) + list(reads):
            if b is owner:
                part = p
                break
        if not hasattr(owner, 'dsems'):
            owner.dsems = {}
        if part not in owner.dsems:
            self.uid += 1
            owner.dsems[part] = [self.semstack.enter_context(self.nc.semaphore('d%d' % self.uid)), 0]
            self.dma_bufs.append((owner, part))
        ent = owner.dsems[part]
        inst = self.eng[q].dma_start(out=out_ap, in_=in_ap, **kw)
        ent[1] += 16
        inst.then_inc(ent[0], 16)
        self.n_inst += 1
        ref = ('d%s/%s' % (owner.name, part), q, ent[0], ent[1])
        for b, p in reads:
            b.record(p, 'r', ref)
        for b, p in writes:
            b.record(p, 'w', ref)
        return inst

    def barrier(self):
        for e in self.ENG:
            for (b, p) in self.dma_bufs:
                ent = b.dsems[p]
                self._wait(e, ('d%s/%s' % (b.name, p), e, ent[0], ent[1]))
            for x in self.ENG:
                if x != e and self.cnt[x] > 0:
                    self._wait(e, ('c' + x, x, self.sem[x], self.cnt[x]))

    def finish(self, e='sp'):
        for (b, p) in self.dma_bufs:
            ent = b.dsems[p]
            self._wait(e, ('d%s/%s' % (b.name, p), e, ent[0], ent[1]))
        for x in self.ENG:
            if x != e and self.cnt[x] > 0:
                self._wait(e, ('c' + x, x, self.sem[x], self.cnt[x]))


class K:
    pass


def build(debug=None):
    nc = bass.Bass("TRN2", target_bir_lowering=False)
    st = ExitStack()
    with st:
        P = Prog(nc, st)
        _build(P, debug)
        print("kernel: n_inst", P.n_inst, {e: P.cnt[e] for e in P.ENG}, flush=True)
    return nc


def _build(P, debug):
    nc = P.nc
    TB = NCH * 128
    NB = NMAIN // TB
    din = lambda n, s, dt=F32: P.dram(n, s, dt, "ExternalInput")
    dout = lambda n, s, dt=F32: P.dram(n, s, dt, "ExternalOutput")

    x_p = din("x_p", [NMAIN, D])
    x_pre = din("x_pre", [NPRE, D])
    g_mk = din("g_mk", [2, 4, NPRE])
    x_s = din("x_s", [NSAMP, D])
    s_re = din("s_re", [NSAMP, G * PST])
    s_im = din("s_im", [NSAMP, G * PST])
    s_c = din("s_c", [NSAMP, H, DK, DV])
    s_n = din("s_n", [NSAMP, H * DK])
    s_mT = din("s_mT", [H, NSAMP])
    w_in = din("w_in", [D, NIN])
    b_if = din("b_if", [H, 2])
    a_re = din("a_re", [128, 32])
    a_im = din("a_im", [128, 32])
    l_dt = din("l_dt", [128, 32])
    bp_re = din("bp_re", [128, 32, 128])
    bp_im = din("bp_im", [128, 32, 128])
    cp_re = din("cp_re", [128, 32, 128])
    cp_im = din("cp_im", [128, 32, 128])
    d_sk = din("d_sk", [128, 8])
    w_glu = din("w_glu", [1024, 1024])
    b_glu = din("b_glu", [128, 8])
    w_aup = din("w_aup", [1024, D])
    mhg = din("mhg", [128, 8])
    w_bup = din("w_bup", [1024, D])
    w_out = din("w_out", [D, D])
    ln_gb = din("ln_gb", [4, D])
    w_gate = din("w_gate", [D, DFF])
    w_up = din("w_up", [D, DFF])
    w_down = din("w_down", [DFF, D])
    c_id = din("c_id", [128, 128])
    c_nm = din("c_nm", [128, 128])
    c_sel = din("c_sel", [4, 4 * 128])
    c_j = din("c_j", [128, 130])
    c_em = din("c_em", [128, NSAMP * NSAMP])

    y_p = dout("y_p", [NMAIN, D])
    y_s = dout("y_s", [NSAMP, D])
    o_pre = dout("o_pre", [128, 32])
    o_pim = dout("o_pim", [128, 32])
    o_pc = dout("o_pc", [H, DK, DV])
    o_pn = dout("o_pn", [128, H * 2])
    o_pm = dout("o_pm", [H, 1])
    o_sre = dout("o_sre", [NSAMP, G * PST])
    o_sim = dout("o_sim", [NSAMP, G * PST])
    o_sc = dout("o_sc", [NSAMP, H, DK, DV])
    o_sn = dout("o_sn", [NSAMP, H * DK])
    o_smT = dout("o_smT", [H, NSAMP])
    dbg = dout("dbg", [128, 4096]) if debug else None

    PS = [P.psum("ps%d" % i, [128, 512]) for i in range(6)]
    PSX = P.psum("psx", [128, 512])
    PSB = P.psum("psb", [128, 1024], BF16)
    psi = [0]

    def nps():
        psi[0] = (psi[0] + 1) % len(PS)
        return PS[psi[0]]

    WB = [P.sbuf("wb%d" % i, [128, 16, 512], BF16) for i in range(2)]
    wbi = [0]

    ident = P.sbuf("ident", [128, 128], F32)
    identb = P.sbuf("identb", [128, 128], BF16)
    negmask = P.sbuf("negmask", [128, 128], F32)
    sel = P.sbuf("sel", [4, 4 * 128], F32)
    ones4 = P.sbuf("ones4", [4, 128 * NCH], F32)
    emask = P.sbuf("emask", [128, NSAMP * NSAMP], F32)
    cosT = P.sbuf("cosT", [128, 32, 130], F32)
    sinT = P.sbuf("sinT", [128, 32, 130], F32)
    rdec = P.sbuf("rdec", [128, 32], F32)
    BTre = P.sbuf("BTre", [128, 32, 128], BF16)
    BTim = P.sbuf("BTim", [128, 32, 128], BF16)
    CTre = P.sbuf("CTre", [128, 32, 128], BF16)
    CTnim = P.sbuf("CTnim", [128, 32, 128], BF16)
    dsk = P.sbuf("dsk", [128, 8], F32)
    bglu = P.sbuf("bglu", [128, 8], F32)
    mhgs = P.sbuf("mhgs", [128, 8], F32)
    bif = P.sbuf("bif", [4, 2], F32)
    wif = P.sbuf("wif", [128, 16, 8], BF16)
    hst = P.sbuf("hst", [128, 2, 32], F32)
    Cn = P.sbuf("Cn", [128, H, 2, 257], F32)
    Cnb = P.sbuf("Cnb", [128, H, 2, 257], BF16)
    Fc = P.sbuf("Fc", [4, 1], F32)
    Mext = P.sbuf("Mext", [4, TB + 1], F32)
    negMp = P.sbuf("negMp", [128, H], F32)

    wcache = {}

    def load_w(wd, r0, kc, c0, ncols, buf=None):
        if buf is None:
            wbi[0] = (wbi[0] + 1) % len(WB)
            b = WB[wbi[0]]
        else:
            b = buf
        key = (wd.name, r0, kc, c0, ncols)
        if key in wcache:
            scr = wcache[key]
            P.dma('sp', b[:, :kc, :ncols], scr.t[:, :].rearrange("p (k n) -> p k n", k=kc), reads=[scr], writes=[b], owner=b)
        else:
            src = wd.t[r0:r0 + 128 * kc, c0:c0 + ncols].rearrange("(k p) n -> p k n", p=128)
            P.dma('pool', b[:, :kc, :ncols], src, writes=[b], owner=b)
            scr = P.dram("wc%d" % len(wcache), [128, kc * ncols], BF16, "Internal")
            wcache[key] = scr
            P.dma('sp', scr.t[:, :].rearrange("p (k n) -> p k n", k=kc), b[:, :kc, :ncols], reads=[b], writes=[scr], owner=b)
        return b

    def mm(ps_ap, lhsT, rhs, start, stop, reads, psbuf):
        P.op('pe', lambda e: e.matmul(ps_ap, lhsT, rhs, start=start, stop=stop), reads=reads, writes=[psbuf])

    def act(out_ap, in_ap, func, reads, writes, **kw):
        P.op('act', lambda e: e.activation(out_ap, in_ap, func, **kw), reads=reads, writes=writes)

    def tt(eng, out_ap, a, b, op, reads, writes):
        P.op(eng, lambda e: e.tensor_tensor(out_ap, a, b, op), reads=reads, writes=writes)

    def ts(eng, out_ap, a, s1, s2, op0, op1, reads, writes):
        if op1 is None:
            P.op(eng, lambda e: e.tensor_scalar(out_ap, a, s1, None, op0), reads=reads, writes=writes)
        else:
            P.op(eng, lambda e: e.tensor_scalar(out_ap, a, s1, s2, op0, op1), reads=reads, writes=writes)

    def stt(eng, out_ap, a, s, b, op0, op1, reads, writes):
        P.op(eng, lambda e: e.scalar_tensor_tensor(out_ap, a, s, b, op0, op1), reads=reads, writes=writes)

    def cp(eng, out_ap, in_ap, reads, writes):
        P.op(eng, lambda e: e.tensor_copy(out_ap, in_ap), reads=reads, writes=writes)

    def memset(eng, buf, ap, val):
        P.op(eng, lambda e: e.memset(ap, val), writes=[buf])

    def sdma(out_ap, in_ap, reads=(), writes=(), owner=None, **kw):
        P.dma('sp', out_ap, in_ap, reads=reads, writes=writes, owner=owner, **kw)

    def dump(buf, ap, ncol, np_=128, c0=0):
        sdma(dbg.t[:np_, c0:c0 + ncol], ap, reads=[buf], writes=[dbg], owner=buf)

    sdma(ident[:], c_id.t[:], writes=[ident], owner=ident)
    sdma(negmask[:], c_nm.t[:], writes=[negmask], owner=negmask)
    sdma(sel[:], c_sel.t[:], writes=[sel], owner=sel)
    sdma(emask[:], c_em.t[:], writes=[emask], owner=emask)
    sdma(dsk[:], d_sk.t[:], writes=[dsk], owner=dsk)
    sdma(bglu[:], b_glu.t[:], writes=[bglu], owner=bglu)
    sdma(mhgs[:], mhg.t[:], writes=[mhgs], owner=mhgs)
    sdma(bif[:], b_if.t[:], writes=[bif], owner=bif)
    cp('dve', identb[:], ident[:], [ident], [identb])
    memset('dve', ones4, ones4[:], 1.0)
    epsc = P.sbuf("epsc", [128, 1], F32)
    memset('dve', epsc, epsc[:], EPS)
    wsrc = w_in.t[:, C_IF:C_IF + 8].rearrange("(k p) n -> p k n", p=128)
    with nc.allow_non_contiguous_dma(reason="tiny gate weight columns"):
        P.dma('pool', wif[:], wsrc, writes=[wif], owner=wif)

    st2 = ExitStack()
    P.stack = st2
    s5t = [P.sbuf("s5t%d" % i, [128, 32], F32) for i in range(6)]
    are = P.sbuf("are", [128, 32], F32)
    aim = P.sbuf("aim", [128, 32], F32)
    ldt = P.sbuf("ldt", [128, 32], F32)
    th = P.sbuf("th", [128, 32], F32)
    jidx = P.sbuf("jidx", [128, 130], F32)
    sdma(are[:], a_re.t[:], writes=[are], owner=are)
    sdma(aim[:], a_im.t[:], writes=[aim], owner=aim)
    sdma(ldt[:], l_dt.t[:], writes=[ldt], owner=ldt)
    sdma(jidx[:], c_j.t[:], writes=[jidx], owner=jidx)
    act(ldt[:], ldt[:], AF.Exp, [ldt], [ldt])
    tt('dve', th[:], aim[:], ldt[:], ALU.mult, [aim, ldt], [th])
    tt('dve', rdec[:], are[:], ldt[:], ALU.mult, [are, ldt], [rdec])
    act(rdec[:], rdec[:], AF.Exp, [rdec], [rdec])
    st3 = ExitStack()
    P.stack = st3
    angb = P.sbuf("angb", [128, 32, 130], F32)
    kfi = P.sbuf("kfi", [128, 32, 130], I32)
    kfb = P.sbuf("kfb", [128, 32, 130], F32)
    for t in range(32):
        ts('dve', angb[:, t, :], jidx[:], th[:, t:t + 1], None, ALU.mult, None, [jidx, th], [(angb, t)])
    TWO_PI = 2.0 * math.pi
    C1 = 6.28125
    C2 = TWO_PI - C1

    def reduce_and_sin(dst, shift):
        ts('dve', kfb[:], angb[:], shift, 1.0 / TWO_PI, ALU.add, ALU.mult, [angb], [kfb])
        cp('dve', kfi[:], kfb[:], [kfb], [kfi])
        cp('dve', kfb[:], kfi[:], [kfi], [kfb])
        tmp = dst
        ts('dve', tmp[:], angb[:], shift, None, ALU.add, None, [angb], [tmp])
        stt('dve', tmp[:], kfb[:], -C1, tmp[:], ALU.mult, ALU.add, [kfb, tmp], [tmp])
        stt('dve', tmp[:], kfb[:], -C2, tmp[:], ALU.mult, ALU.add, [kfb, tmp], [tmp])
        ts('dve', kfb[:], tmp[:], math.pi, TWO_PI, ALU.is_gt, ALU.mult, [tmp], [kfb])
        tt('dve', tmp[:], tmp[:], kfb[:], ALU.subtract, [tmp, kfb], [tmp])
        ts('dve', kfb[:], tmp[:], -math.pi, TWO_PI, ALU.is_lt, ALU.mult, [tmp], [kfb])
        tt('dve', tmp[:], tmp[:], kfb[:], ALU.add, [tmp, kfb], [tmp])
        ts('dve', tmp[:], tmp[:], math.pi, -math.pi, ALU.min, ALU.max, [tmp], [tmp])
        act(dst[:], tmp[:], AF.Sin, [tmp], [dst])

    reduce_and_sin(sinT, 0.0)
    reduce_and_sin(cosT, math.pi / 2.0)
    P.barrier()
    st3.close()
    P.stack = st2

    c1 = cosT[:, :, 1]
    s1 = sinT[:, :, 1]
    nre, nim, den, cre, cim, t5 = s5t
    tt('dve', nre[:], rdec[:], c1, ALU.mult, [rdec, cosT], [nre])
    ts('dve', nre[:], nre[:], -1.0, None, ALU.add, None, [nre], [nre])
    tt('dve', nim[:], rdec[:], s1, ALU.mult, [rdec, sinT], [nim])
    tt('dve', den[:], are[:], are[:], ALU.mult, [are], [den])
    tt('dve', t5[:], aim[:], aim[:], ALU.mult, [aim], [t5])
    tt('dve', den[:], den[:], t5[:], ALU.add, [den, t5], [den])
    P.op('dve', lambda e: e.reciprocal(den[:], den[:]), reads=[den], writes=[den])
    tt('dve', cre[:], nre[:], are[:], ALU.mult, [nre, are], [cre])
    tt('dve', t5[:], nim[:], aim[:], ALU.mult, [nim, aim], [t5])
    tt('dve', cre[:], cre[:], t5[:], ALU.add, [cre, t5], [cre])
    tt('dve', cre[:], cre[:], den[:], ALU.mult, [cre, den], [cre])
    tt('dve', cim[:], nim[:], are[:], ALU.mult, [nim, are], [cim])
    tt('dve', t5[:], nre[:], aim[:], ALU.mult, [nre, aim], [t5])
    tt('dve', cim[:], cim[:], t5[:], ALU.subtract, [cim, t5], [cim])
    tt('dve', cim[:], cim[:], den[:], ALU.mult, [cim, den], [cim])
    Xa = P.sbuf("Xa", [128, 32, 128], F32)
    Xb = P.sbuf("Xb", [128, 32, 128], F32)
    Xc = P.sbuf("Xc", [128, 32, 128], F32)
    Xd = P.sbuf("Xd", [128, 32, 128], F32)
    sdma(Xa[:], bp_re.t[:], writes=[Xa], owner=Xa)
    sdma(Xb[:], bp_im.t[:], writes=[Xb], owner=Xb)
    for t in range(32):
        ts('dve', Xc[:, t, :], Xa[:, t, :], cre[:, t:t + 1], None, ALU.mult, None, [(Xa, t), cre], [(Xc, t)])
        ts('dve', Xd[:, t, :], Xb[:, t, :], cre[:, t:t + 1], None, ALU.mult, None, [(Xb, t), cre], [(Xd, t)])
        ts('dve', Xb[:, t, :], Xb[:, t, :], cim[:, t:t + 1], None, ALU.mult, None, [(Xb, t), cim], [(Xb, t)])
        ts('dve', Xa[:, t, :], Xa[:, t, :], cim[:, t:t + 1], None, ALU.mult, None, [(Xa, t), cim], [(Xa, t)])
        tt('dve', Xc[:, t, :], Xc[:, t, :], Xb[:, t, :], ALU.subtract, [(Xc, t), (Xb, t)], [(Xc, t)])
        tt('dve', Xd[:, t, :], Xd[:, t, :], Xa[:, t, :], ALU.add, [(Xd, t), (Xa, t)], [(Xd, t)])

    def transpose_tiles(src, dst, scale=None):
        for t4 in range(8):
            ps = nps()
            for i in range(4):
                t = t4 * 4 + i
                P.op('pe', lambda e, t=t, i=i, ps=ps: e.transpose(ps[:, i * 128:(i + 1) * 128], src[:, t, :], ident[:]),
                     reads=[(src, t), ident], writes=[ps])
            o = dst[:, t4 * 4:(t4 + 1) * 4, :]
            pin = ps[:, :].rearrange("p (a b) -> p a b", a=4)
            if scale is None:
                act(o, pin, AF.Copy, [ps], [dst])
            else:
                act(o, pin, AF.Copy, [ps], [dst], scale=scale)

    transpose_tiles(Xc, BTre)
    transpose_tiles(Xd, BTim)
    sdma(Xa[:], cp_re.t[:], writes=[Xa], owner=Xa)
    sdma(Xb[:], cp_im.t[:], writes=[Xb], owner=Xb)
    transpose_tiles(Xa, CTre)
    transpose_tiles(Xb, CTnim, scale=-1.0)

    P.barrier()
    st2.close()
    P.stack = P.semstack
    s5t = [P.sbuf("s5u%d" % i, [128, 32], F32) for i in range(6)]
    xtok = P.sbuf("xtok", [128, NCH, D], F32)
    xT = P.sbuf("xT", [128, 16, TB], BF16)
    gT = P.sbuf("gT", [128, 8, TB], BF16)
    ybT = P.sbuf("ybT", [128, 8, TB], BF16)
    colsb = P.sbuf("colsb", [128, 16], F32)
    carr = P.sbuf("carr", [4, 8], F32)
    carb = P.sbuf("carb", [128, H], F32)
    glast = P.sbuf("glast", [128, 2, 32], F32)
    sm = P.sbuf("sm", [128, 16], F32)
    lnst = P.sbuf("lnst", [128, 4, 6], F32)
    B = K()
    scope = [None]
    tagc = [0]

    scopes = []

    def open_scope():
        scopes.append(ExitStack())
        P.stack = scopes[-1]
        tagc[0] += 1
        return "_%d" % tagc[0]

    def close_scope():
        P.barrier()
        scopes.pop().close()
        P.stack = scopes[-1] if scopes else P.semstack

    def alloc_A(tb, state_only=False, sample=False):
        t = open_scope()
        nch = max(1, tb // 128)
        B.uT = P.sbuf("uT" + t, [128, 8, tb], BF16)
        if not state_only:
            B.qT = P.sbuf("qT" + t, [128, 8, tb], BF16)
            if not sample:
                B.kT = P.sbuf("kT" + t, [128, 8, tb], BF16)
            B.sigo = P.sbuf("sigo" + t, [128, 8, tb], BF16)
        B.v1 = P.sbuf("v1" + t, [128, nch, H, 257], BF16)
        B.ktok = P.sbuf("ktok" + t, [128, nch, 1024], BF16)
        for nm in ("igr", "fpr", "Frow", "Arow", "negM", "clampa"):
            setattr(B, nm, P.sbuf(nm + t, [4, tb], F32))
        B.R4 = P.sbuf("R4" + t, [4, 4, 128], F32)
        if state_only:
            B.gmk = P.sbuf("gmk" + t, [4, 2, tb], F32)
        for nm in ("vre", "vim", "tA", "tB", "gre", "gim"):
            setattr(B, nm, P.sbuf(nm + t, [128, 4, 128], F32))
        if state_only or sample:
            B.gre2, B.gim2 = [B.gre, B.gre], [B.gim, B.gim]
        else:
            B.gre2 = [B.gre, P.sbuf("greB" + t, [128, 4, 128], F32)]
            B.gim2 = [B.gim, P.sbuf("gimB" + t, [128, 4, 128], F32)]

        B.pr = [P.sbuf("pr%d" % i + t, [128, 4, 128], BF16) for i in range(4)]
        if not state_only:
            B.ysk = P.sbuf("ysk" + t, [128, 128], F32)
            if not sample:
                B.Wsb = P.sbuf("Wsb" + t, [128, 128], F32)
                B.PTb = P.sbuf("PTb" + t, [128, 128], BF16)
            B.tmpi = P.sbuf("tmpi" + t, [128, 257], F32)
            B.nd = P.sbuf("nd" + t, [128, 257], F32)
            B.hh = P.sbuf("hh" + t, [128, 256], F32)
            B.hnb = P.sbuf("hnb" + t, [128, 256], BF16)
        B.kwt = P.sbuf("kwt" + t, [128, 256], BF16)
        memset('pool', B.v1, B.v1[:], 1.0)

    def alloc_B(tb):
        t = open_scope()
        B.yaT = P.sbuf("yaT" + t, [128, 8, tb], BF16)
        B.mixT = P.sbuf("mixT" + t, [128, 16, tb], BF16)
        B.sgate = P.sbuf("sgate" + t, [128, 4, tb], F32)
        B.sgb = P.sbuf("sgb" + t, [128, 4, tb], BF16)

    def alloc_C(tb):
        t = open_scope()
        B.hT = P.sbuf("hT" + t, [128, 12, tb], BF16)
        B.sgate = P.sbuf("sgate" + t, [128, 4, tb], F32)

    memset('dve', hst, hst[:], 0.0)
    memset('dve', Cn, Cn[:], 0.0)
    memset('dve', Cnb, Cnb[:], 0.0)
    memset('dve', Fc, Fc[:], 0.0)
    memset('dve', Mext, Mext[:], 0.0)
    memset('dve', negMp, negMp[:], 0.0)

    def load_x(tiles):
        for i, (L, src, _) in enumerate(tiles):
            sdma(xtok[:L, i, :], src, writes=[(xtok, i)], owner=xtok)

    def make_xT(src, tiles, dstT, gbi=None):
        off = 0
        for i, (L, _, _) in enumerate(tiles):
            for k4 in range(4):
                ps = nps()
                for kk in range(4):
                    k = k4 * 4 + kk
                    P.op('pe', lambda e, k=k, kk=kk, ps=ps, L=L, i=i: e.transpose(
                        ps[:, kk * 128:kk * 128 + L], src[:L, i, k * 128:(k + 1) * 128], ident[:L, :L]),
                        reads=[(src, i), ident], writes=[ps])
                pin = ps[:, :].rearrange("p (a b) -> p a b", a=4)[:, :, :L]
                o = dstT[:, k4 * 4:(k4 + 1) * 4, off:off + L]
                eng = 'act' if (k4 % 2 == 0) else 'dve'
                if eng == 'act':
                    act(o, pin, AF.Copy, [ps], [dstT])
                else:
                    cp('dve', o, pin, [ps], [dstT])
            off += L
        return off

    def projF(wb, kc, f0, M, rhsT, ntok, ps):
        for k in range(kc):
            mm(ps[:M, :ntok], wb[:, k, f0:f0 + M], rhsT[:, k, :ntok], k == 0, k == kc - 1, [wb, rhsT], ps)

    def in_proj(tiles, ntok, need, mask_off=None):
        def grp(c0, dstT=None, fkind=None, tkind=None):
            for g in range(2):
                wb = load_w(w_in, 0, 16, c0 + g * 512, 512)
                if fkind is not None:
                    for ft in range(4):
                        ps = nps()
                        projF(wb, 16, ft * 128, 128, xT, ntok, ps)
                        o = dstT[:, g * 4 + ft, :ntok]
                        if fkind == 'copy':
                            act(o, ps[:, :ntok], AF.Copy, [ps], [dstT])
                        elif fkind == 'k':
                            act(o, ps[:, :ntok], AF.Copy, [ps], [dstT], scale=1.0 / 16.0)
                        elif fkind == 'sig':
                            act(o, ps[:, :ntok], AF.Sigmoid, [ps], [dstT])
                if tkind is not None:
                    off = 0
                    for i, (L, _, _) in enumerate(tiles):
                        ps = nps()
                        for k in range(16):
                            mm(ps[:L, :512], xT[:, k, off:off + L], wb[:, k, :512], k == 0, k == 15, [wb, xT], ps)
                        if tkind == 'q':
                            act(B.qtk[:L, g * 512:(g + 1) * 512], ps[:L, :512], AF.Copy, [ps], [B.qtk])
                        elif tkind == 'v':
                            for hh_ in range(2):
                                h = g * 2 + hh_
                                cp('dve', B.v1[:L, i, h, 0:256], ps[:L, hh_ * 256:(hh_ + 1) * 256], [ps], [(B.v1, i)])
                        else:
                            act(B.ktok[:L, i, g * 512:(g + 1) * 512], ps[:L, :512], AF.Copy, [ps], [(B.ktok, i)], scale=1.0 / 16.0)
                        off += L

        if 'u' in need:
            grp(C_U, B.uT, 'copy')
        if 'q' in need or 'qtok' in need:
            grp(C_Q, B.qT if 'q' in need else None, 'copy' if 'q' in need else None, 'q' if 'qtok' in need else None)
        if 'kF' in need or 'k' in need:
            grp(C_K, B.kT if 'kF' in need else None, 'k' if 'kF' in need else None, 'k' if 'k' in need else None)
        if 'v' in need:
            grp(C_V, None, None, 'v')
        if 'o' in need:
            grp(C_O, B.sigo, 'sig')
        for j, dst in enumerate((B.igr, B.fpr)):
            ps = nps()
            for k in range(16):
                mm(ps[:4, :ntok], wif[:, k, j * 4:(j + 1) * 4], xT[:, k, :ntok], k == 0, k == 15, [wif, xT], ps)
            act(dst[:, :ntok], ps[:4, :ntok], AF.Identity, [ps], [dst], bias=bif[:, j:j + 1])
            if mask_off is not None:
                sdma(B.gmk[:, j, :ntok], g_mk.t[j, :, mask_off:mask_off + ntok], writes=[(B.gmk, j)], owner=B.gmk)
                tt('dve', dst[:, :ntok], dst[:, :ntok], B.gmk[:, j, :ntok], ALU.add, [dst, (B.gmk, j)], [dst])

    def gate_rows(ntok):
        act(B.fpr[:, :ntok], B.fpr[:, :ntok], AF.Exp, [B.fpr], [B.fpr], scale=-1.0)
        act(B.fpr[:, :ntok], B.fpr[:, :ntok], AF.Ln, [B.fpr], [B.fpr], bias=1.0)
        P.op('dve', lambda e: e.tensor_tensor_scan(B.Frow[:, :ntok], ones4[:, :ntok], B.fpr[:, :ntok], Fc[:, 0:1],
                                                   ALU.mult, ALU.subtract), reads=[ones4, B.fpr, Fc], writes=[B.Frow])
        tt('dve', B.Arow[:, :ntok], B.igr[:, :ntok], B.Frow[:, :ntok], ALU.subtract, [B.igr, B.Frow], [B.Arow])
        P.op('dve', lambda e: e.tensor_tensor_scan(Mext[:, 1:ntok + 1], B.Arow[:, :ntok], B.Arow[:, :ntok], Mext[:, 0:1],
                                                   ALU.max, ALU.max), reads=[B.Arow, (Mext, 0)], writes=[(Mext, 1)])
        ts('dve', B.negM[:, :ntok], Mext[:, 1:ntok + 1], -1.0, None, ALU.mult, None, [(Mext, 1)], [B.negM])
        stt('dve', B.clampa[:, :ntok], B.Frow[:, :ntok], -1.0, Mext[:, 1:ntok + 1], ALU.mult, ALU.subtract,
            [B.Frow, (Mext, 1)], [B.clampa])

    def gate_rows_end(ntok):
        cp('dve', Fc[:, 0:1], B.Frow[:, ntok - 1:ntok], [B.Frow], [Fc])
        cp('dve', Mext[:, 0:1], Mext[:, ntok:ntok + 1], [(Mext, 1)], [(Mext, 0)])

    def s5_chunk(off, L, full):
        def stage_a(ct):
            gre_, gim_ = B.gre2[ct % 2], B.gim2[ct % 2]
            psr = nps()
            psi_ = nps()
            for i in range(4):
                t = ct * 4 + i
                mm(psr[:, i * 128:i * 128 + L], BTre[:, t, :], B.uT[:, ct, off:off + L], True, True, [BTre, B.uT], psr)
                mm(psi_[:, i * 128:i * 128 + L], BTim[:, t, :], B.uT[:, ct, off:off + L], True, True, [BTim, B.uT], psi_)
            pr3 = psr[:, :].rearrange("p (a b) -> p a b", a=4)[:, :, :L]
            pi3 = psi_[:, :].rearrange("p (a b) -> p a b", a=4)[:, :, :L]
            cs = cosT[:, ct * 4:(ct + 1) * 4, 0:L]
            sn = sinT[:, ct * 4:(ct + 1) * 4, 0:L]
            tt('dve', B.vre[:, :, :L], pr3, cs, ALU.mult, [psr, cosT], [B.vre])
            tt('dve', B.tA[:, :, :L], pi3, sn, ALU.mult, [psi_, sinT], [B.tA])
            tt('dve', B.vim[:, :, :L], pi3, cs, ALU.mult, [psi_, cosT], [B.vim])
            tt('dve', B.tB[:, :, :L], pr3, sn, ALU.mult, [psr, sinT], [B.tB])
            tt('dve', B.vre[:, :, :L], B.vre[:, :, :L], B.tA[:, :, :L], ALU.add, [B.vre, B.tA], [B.vre])
            tt('dve', B.vim[:, :, :L], B.vim[:, :, :L], B.tB[:, :, :L], ALU.subtract, [B.vim, B.tB], [B.vim])
            for i in range(4):
                t = ct * 4 + i
                for (vv, gg, ri) in ((B.vre, gre_, 0), (B.vim, gim_, 1)):
                    P.op('dve', lambda e, vv=vv, gg=gg, ri=ri, t=t, i=i: e.tensor_tensor_scan(
                        gg[:, i, :L], rdec[:, t:t + 1].to_broadcast([128, L]), vv[:, i, :L], hst[:, ri, t:t + 1],
                        ALU.mult, ALU.add), reads=[rdec, vv, (hst, ct)], writes=[gg])
            cp('pool', glast[:, 0, ct * 4:(ct + 1) * 4], gre_[:, :, L - 1], [gre_], [(glast, ct)])
            cp('pool', glast[:, 1, ct * 4:(ct + 1) * 4], gim_[:, :, L - 1], [gim_], [(glast, ct)])

        def stage_b(ct):
            gre_, gim_ = B.gre2[ct % 2], B.gim2[ct % 2]
            cs = cosT[:, ct * 4:(ct + 1) * 4, 0:L]
            sn = sinT[:, ct * 4:(ct + 1) * 4, 0:L]
            gr = gre_[:, :, :L]
            gi = gim_[:, :, :L]
            tt('pool', B.pr[0][:, :, :L], gr, cs, ALU.mult, [gre_, cosT], [B.pr[0]])
            tt('pool', B.pr[2][:, :, :L], gr, sn, ALU.mult, [gre_, sinT], [B.pr[2]])
            tt('pool', B.pr[3][:, :, :L], gi, cs, ALU.mult, [gim_, cosT], [B.pr[3]])
            stt('dve', B.pr[1][:, :, :L], gi, -1.0, sn, ALU.mult, ALU.mult, [gim_, sinT], [B.pr[1]])

        def stage_c(ct):
            psy = nps()
            n = 0
            for i in range(4):
                t = ct * 4 + i
                for j, CT in enumerate((CTre, CTre, CTnim, CTnim)):
                    mm(psy[:, :L], CT[:, t, :], B.pr[j][:, i, :L], n == 0, n == 15, [CT, B.pr[j]], psy)
                    n += 1
            stt('dve', gT[:, ct, off:off + L], B.uT[:, ct, off:off + L], dsk[:, ct:ct + 1], psy[:, :L], ALU.mult, ALU.add,
                [B.uT, dsk, psy], [gT])

        stage_a(0)
        for ct in range(8):
            if full:
                stage_b(ct)
            if ct + 1 < 8:
                stage_a(ct + 1)
            if full:
                stage_c(ct)
            yield
        cL = cosT[:, :, L]
        sL = sinT[:, :, L]
        glr = glast[:, 0, :]
        gli = glast[:, 1, :]
        a0, a1, a2, a3 = s5t[0], s5t[1], s5t[2], s5t[3]
        tt('dve', a0[:], glr, cL, ALU.mult, [glast, cosT], [a0])
        tt('dve', a1[:], gli, sL, ALU.mult, [glast, sinT], [a1])
        tt('dve', a2[:], glr, sL, ALU.mult, [glast, sinT], [a2])
        tt('dve', a3[:], gli, cL, ALU.mult, [glast, cosT], [a3])
        tt('dve', hst[:, 0, :], a0[:], a1[:], ALU.subtract, [a0, a1], [hst])
        tt('dve', hst[:, 1, :], a2[:], a3[:], ALU.add, [a2, a3], [hst])
        yield

    def s5_state_out(L, dre, dim_):
        cL = cosT[:, :, L - 1]
        sL = sinT[:, :, L - 1]
        glr = glast[:, 0, :]
        gli = glast[:, 1, :]
        a0, a1, a2, a3, a4, a5 = s5t
        tt('dve', a0[:], glr, cL, ALU.mult, [glast, cosT], [a0])
        tt('dve', a1[:], gli, sL, ALU.mult, [glast, sinT], [a1])
        tt('dve', a2[:], glr, sL, ALU.mult, [glast, sinT], [a2])
        tt('dve', a3[:], gli, cL, ALU.mult, [glast, cosT], [a3])
        tt('dve', a4[:], a0[:], a1[:], ALU.subtract, [a0, a1], [a4])
        tt('dve', a5[:], a2[:], a3[:], ALU.add, [a2, a3], [a5])
        sdma(dre.t[:], a4[:], reads=[a4], writes=[dre], owner=a4)
        sdma(dim_.t[:], a5[:], reads=[a5], writes=[dim_], owner=a5)

    def mlstm_out(h, off, L):
        stt('dve', sm[:L, 6:7], B.nd[:L, 256:257], -1.0, B.nd[:L, 256:257], ALU.mult, ALU.max, [B.nd], [(sm, 6)])
        tt('dve', sm[:L, 0:1], sm[:L, 6:7], colsb[:L, 8 + h:9 + h], ALU.max, [(sm, 6), colsb], [(sm, 0)])
        P.op('dve', lambda e: e.reciprocal(sm[:L, 1:2], sm[:L, 0:1]), reads=[(sm, 0)], writes=[(sm, 1)])
        ts('dve', B.hh[:L, :], B.nd[:L, 0:256], sm[:L, 1:2], None, ALU.mult, None, [B.nd, (sm, 1)], [B.hh])
        P.op('dve', lambda e: e.bn_stats(lnst[:L, 0, :], B.hh[:L, :]), reads=[B.hh], writes=[lnst])
        P.op('dve', lambda e: e.bn_aggr(sm[:L, 2:4], lnst[:L, 0, :]), reads=[lnst], writes=[(sm, 2)])
        act(sm[:L, 4:5], sm[:L, 3:4], AF.Ln, [(sm, 2)], [(sm, 4)], bias=epsc[:L, 0:1])
        act(sm[:L, 5:6], sm[:L, 4:5], AF.Exp, [(sm, 4)], [(sm, 5)], scale=-0.5)
        ts('dve', B.hnb[:L, :], B.hh[:L, :], sm[:L, 2:3], sm[:L, 5:6], ALU.subtract, ALU.mult, [B.hh, (sm, 2), (sm, 5)], [B.hnb])
        for c in range(2):
            P.op('pe', lambda e, c=c: e.transpose(PSB[:, c * 128:c * 128 + L], B.hnb[:L, c * 128:(c + 1) * 128], identb[:L, :L]),
                 reads=[B.hnb, identb], writes=[PSB])
            stt('dve', ybT[:, 2 * h + c, off:off + L], PSB[:, c * 128:c * 128 + L], mhgs[:, 2 * h + c:2 * h + c + 1],
                B.sigo[:, 2 * h + c, off:off + L], ALU.mult, ALU.mult, [PSB, mhgs, B.sigo], [ybT])

    def mlstm_chunk(ci, off, L, full):
        Me = Mext[:, off:off + 1]
        ME = Mext[:, off + L:off + L + 1]
        Mpart = (Mext, 1)
        ts('dve', B.R4[:, 0, :L], Mext[:, off + 1:off + L + 1], -1.0, Me, ALU.mult, ALU.add, [Mext], [(B.R4, 0)])
        ts('dve', B.R4[:, 1, :L], B.Arow[:, off:off + L], ME, None, ALU.subtract, None, [B.Arow, Mext], [(B.R4, 1)])
        cp('dve', B.R4[:, 2, :L], B.clampa[:, off:off + L], [B.clampa], [(B.R4, 2)])
        for r in range(3):
            act(B.R4[:, r, :L], B.R4[:, r, :L], AF.Exp, [(B.R4, r)], [(B.R4, r)])
        cp('dve', B.R4[:, 3, :L], B.Arow[:, off:off + L], [B.Arow], [(B.R4, 3)])
        tt('dve', carr[:, 0:1], Me, ME, ALU.subtract, [Mext], [carr])
        act(carr[:, 0:1], carr[:, 0:1], AF.Exp, [carr], [carr])
        ts('dve', carr[:, 4:8], ident[:4, :4], carr[:, 0:1], None, ALU.mult, None, [ident, carr], [carr])
        psc = nps()
        P.op('pe', lambda e: e.matmul(psc[:, :4], ones4[:, :128], carr[:, 4:8], start=True, stop=True),
             reads=[ones4, carr], writes=[psc])
        cp('dve', carb[:, :], psc[:, :4], [psc], [carb])
        pst = nps()
        for r in range(4):
            P.op('pe', lambda e, r=r: e.transpose(pst[:L, r * 4:(r + 1) * 4], B.R4[:, r, :L], ident[:4, :4]),
                 reads=[(B.R4, r), ident], writes=[pst])
        cp('dve', colsb[:L, :], pst[:L, :16], [pst], [colsb])
        yield
        for h in range(H):
            if full:
                psR = nps()
                P.op('pe', lambda e, h=h: e.matmul(psR[:L, :L], sel[:, h * 128:h * 128 + L], B.negM[:, off:off + L],
                                                   start=True, stop=False), reads=[sel, B.negM], writes=[psR])
                P.op('pe', lambda e: e.matmul(psR[:L, :L], ident[:L, :L], negmask[:L, :L], start=False, stop=True),
                     reads=[ident, negmask], writes=[psR])
                psS = nps()
                for c in range(2):
                    mm(psS[:L, :L], B.kT[:, 2 * h + c, off:off + L], B.qT[:, 2 * h + c, off:off + L], c == 0, c == 1, [B.kT, B.qT], psS)
                act(B.Wsb[:L, :L], psR[:L, :L], AF.Exp, [psR, colsb], [B.Wsb], bias=colsb[:L, 12 + h:13 + h])
                tt('dve', B.PTb[:L, :L], psS[:L, :L], B.Wsb[:L, :L], ALU.mult, [psS, B.Wsb], [B.PTb])
                psA = nps()
                mm(psA[:L, :257], B.PTb[:L, :L], B.v1[:L, ci, h, :], True, True, [B.PTb, (B.v1, ci)], psA)
                psE = nps()
                for c in range(2):
                    mm(psE[:L, :257], B.qT[:, 2 * h + c, off:off + L], Cnb[:, h, c, :], c == 0, c == 1, [B.qT, (Cnb, h)], psE)
                act(B.tmpi[:L, :], psA[:L, :257], AF.Copy, [psA], [B.tmpi])
                stt('dve', B.nd[:L, :], psE[:L, :257], colsb[:L, h:h + 1], B.tmpi[:L, :], ALU.mult, ALU.add, [psE, colsb, B.tmpi], [B.nd])
                mlstm_out(h, off, L)
            ts('dve', B.kwt[:L, :], B.ktok[:L, ci, h * 256:(h + 1) * 256], colsb[:L, 4 + h:5 + h], None, ALU.mult, None,
               [(B.ktok, ci), colsb], [B.kwt])
            for c in range(2):
                psU = nps()
                mm(psU[:, :257], B.kwt[:L, c * 128:(c + 1) * 128], B.v1[:L, ci, h, :], True, True, [B.kwt, (B.v1, ci)], psU)
                stt('dve', Cn[:, h, c, :], Cn[:, h, c, :], carb[:, h:h + 1], psU[:, :257], ALU.mult, ALU.add,
                    [(Cn, h), carb, psU], [(Cn, h)])
                act(Cnb[:, h, c, :], Cn[:, h, c, :], AF.Copy, [(Cn, h)], [(Cnb, h)])
            yield

    def layernorm_tile(buf, i, L, gi):
        for c in range(4):
            P.op('dve', lambda e, c=c: e.bn_stats(lnst[:L, c, :], buf[:L, i, c * 512:(c + 1) * 512]),
                 reads=[(buf, i)], writes=[lnst])
        P.op('dve', lambda e: e.bn_aggr(sm[:L, 8:10], lnst[:L, :, :].rearrange("p a b -> p (a b)")), reads=[lnst], writes=[(sm, 8)])
        ts('dve', sm[:L, 10:11], sm[:L, 9:10], EPS, None, ALU.add, None, [(sm, 8)], [(sm, 10)])
        act(sm[:L, 10:11], sm[:L, 10:11], AF.Sqrt, [(sm, 10)], [(sm, 10)])
        P.op('dve', lambda e: e.reciprocal(sm[:L, 11:12], sm[:L, 10:11]), reads=[(sm, 10)], writes=[(sm, 11)])
        ts('dve', buf[:L, i, :], buf[:L, i, :], sm[:L, 8:9], sm[:L, 11:12], ALU.subtract, ALU.mult,
           [(buf, i), (sm, 8), (sm, 11)], [(buf, i)])
        eng = 'dve' if i % 2 == 0 else 'pool'
        tt(eng, buf[:L, i, :], buf[:L, i, :], B.gb[:L, 0, :], ALU.mult, [(buf, i), B.gb], [(buf, i)])
        tt(eng, buf[:L, i, :], buf[:L, i, :], B.gb[:L, 1, :], ALU.add, [(buf, i), B.gb], [(buf, i)])

    def load_gb(gi):
        for j in range(2):
            src = ln_gb.t[gi + j:gi + j + 1, :].to_broadcast([128, D])
            P.dma('pool', B.gb[:, j, :], src, writes=[B.gb], owner=B.gb)

    def tail(tiles, ntok, tb, gelu_pending=True):
        alloc_B(tb)
        if gelu_pending:
            tg = open_scope()
            B.gtmp = P.sbuf("gtmp" + tg, [128, 8, tb], F32)
            yv = gT[:, :, :ntok]
            tm = B.gtmp[:, :, :ntok]
            tt('dve', tm, yv, yv, ALU.mult, [gT], [B.gtmp])
            ts('dve', tm, tm, 0.044715, 1.0, ALU.mult, ALU.add, [B.gtmp], [B.gtmp])
            tt('pool', tm, tm, yv, ALU.mult, [B.gtmp, gT], [B.gtmp])
            act(tm, tm, AF.Sigmoid, [B.gtmp], [B.gtmp], scale=1.5957691216057308)
            tt('dve', yv, yv, tm, ALU.mult, [gT, B.gtmp], [gT])
            close_scope()
        tw = open_scope()
        WB.append(P.sbuf("wb3" + tw, [128, 16, 512], BF16))
        WB.append(P.sbuf("wb4" + tw, [128, 16, 512], BF16))
        for g in range(2):
            wb = load_w(w_glu, 0, 8, g * 512, 512)
            for ft in range(4):
                f = g * 4 + ft
                ps = nps()
                projF(wb, 8, ft * 128, 128, gT, ntok, ps)
                act(B.sgate[:, ft, :ntok], ps[:, :ntok], AF.Sigmoid, [ps, bglu], [(B.sgate, ft)], bias=bglu[:, f:f + 1])
                tt('dve', B.yaT[:, f, :ntok], B.sgate[:, ft, :ntok], gT[:, f, :ntok], ALU.mult, [(B.sgate, ft), gT], [B.yaT])
        for g in range(4):
            wb = load_w(w_in, 0, 16, C_GA + g * 512, 512)
            for ft in range(4):
                ps = nps()
                projF(wb, 16, ft * 128, 128, xT, ntok, ps)
                act(B.sgate[:, ft, :ntok], ps[:, :ntok], AF.Sigmoid, [ps], [(B.sgate, ft)])
            wb = load_w(w_aup, 0, 8, g * 512, 512)
            for ft in range(4):
                ps = nps()
                projF(wb, 8, ft * 128, 128, B.yaT, ntok, ps)
                tt('dve', B.sgate[:, ft, :ntok], ps[:, :ntok], B.sgate[:, ft, :ntok], ALU.mult, [ps, (B.sgate, ft)], [(B.sgate, ft)])
            wb = load_w(w_in, 0, 16, C_GB + g * 512, 512)
            for ft in range(4):
                ps = nps()
                projF(wb, 16, ft * 128, 128, xT, ntok, ps)
                act(B.sgb[:, ft, :ntok], ps[:, :ntok], AF.Sigmoid, [ps], [(B.sgb, ft)])
            wb = load_w(w_bup, 0, 8, g * 512, 512)
            for ft in range(4):
                ps = nps()
                projF(wb, 8, ft * 128, 128, ybT, ntok, ps)
                tt('dve', B.sgb[:, ft, :ntok], ps[:, :ntok], B.sgb[:, ft, :ntok], ALU.mult, [ps, (B.sgb, ft)], [(B.sgb, ft)])
                tt('pool', B.mixT[:, g * 4 + ft, :ntok], B.sgb[:, ft, :ntok], B.sgate[:, ft, :ntok], ALU.add,
                   [(B.sgb, ft), (B.sgate, ft)], [B.mixT])
        for cg in range(4):
            wb = load_w(w_out, 0, 16, cg * 512, 512)
            off = 0
            for i, (L, _, _) in enumerate(tiles):
                ps = nps()
                for k in range(16):
                    mm(ps[:L, :512], B.mixT[:, k, off:off + L], wb[:, k, :512], k == 0, k == 15, [wb, B.mixT], ps)
                stt('dve', xtok[:L, i, cg * 512:(cg + 1) * 512], xtok[:L, i, cg * 512:(cg + 1) * 512], ALPHA, ps[:L, :512],
                    ALU.mult, ALU.add, [(xtok, i), ps], [(xtok, i)])
                off += L
        WB.pop()
        WB.pop()
        close_scope()
        B.gb = P.sbuf("gb_%d" % tagc[0], [128, 2, D], F32)
        load_gb(0)
        for i, (L, _, _) in enumerate(tiles):
            layernorm_tile(xtok, i, L, 0)
        make_xT(xtok, tiles, xT)
        close_scope()
        alloc_C(tb)
        tw = open_scope()
        WB.append(P.sbuf("wb3" + tw, [128, 16, 512], BF16))
        WB.append(P.sbuf("wb4" + tw, [128, 16, 512], BF16))
        parts = [(0, 3), (3, 3), (6, 3), (9, 2)]
        for pi, (g0, ng) in enumerate(parts):
            for gg in range(ng):
                g = g0 + gg
                wbg = load_w(w_gate, 0, 16, g * 512, 512)
                wbu = load_w(w_up, 0, 16, g * 512, 512)
                for ft in range(4):
                    ps = nps()
                    projF(wbg, 16, ft * 128, 128, xT, ntok, ps)
                    act(B.sgate[:, ft, :ntok], ps[:, :ntok], AF.Silu, [ps], [(B.sgate, ft)])
                for ft in range(4):
                    ps = nps()
                    projF(wbu, 16, ft * 128, 128, xT, ntok, ps)
                    tt('dve', B.hT[:, gg * 4 + ft, :ntok], ps[:, :ntok], B.sgate[:, ft, :ntok], ALU.mult,
                       [ps, (B.sgate, ft)], [(B.hT, gg * 4 + ft)])
            kc = ng * 4
            for cg in range(4):
                wb = load_w(w_down, g0 * 512, kc, cg * 512, 512)
                off = 0
                for i, (L, _, _) in enumerate(tiles):
                    ps = nps()
                    for k in range(kc):
                        mm(ps[:L, :512], B.hT[:, k, off:off + L], wb[:, k, :512], k == 0, k == kc - 1, [wb, (B.hT, k)], ps)
                    stt('dve', xtok[:L, i, cg * 512:(cg + 1) * 512], xtok[:L, i, cg * 512:(cg + 1) * 512],
                        ALPHA if pi == 0 else 1.0, ps[:L, :512], ALU.mult, ALU.add, [(xtok, i), ps], [(xtok, i)])
                    off += L
        WB.pop()
        WB.pop()
        close_scope()
        B.gb = P.sbuf("gb_%d" % tagc[0], [128, 2, D], F32)
        load_gb(2)
        for i, (L, _, dst) in enumerate(tiles):
            layernorm_tile(xtok, i, L, 2)
            sdma(dst, xtok[:L, i, :], reads=[(xtok, i)], writes=[], owner=xtok)
        close_scope()

    def run_sample_block():
        L = NSAMP
        tiles = [(L, x_s.t[:, :], y_s.t[:, :])]
        alloc_A(128, sample=True)
        B.qtk = P.sbuf("s_qtk", [16, 1024], BF16)
        B.nst = P.sbuf("s_nst", [16, 1024], F32)
        B.carbS = P.sbuf("s_carbS", [128, 64], F32)
        B.Cbb = P.sbuf("s_Cbb", [128, 2, 256], BF16)
        load_x(tiles)
        make_xT(xtok, tiles, xT)
        in_proj(tiles, L, {'u', 'q', 'qtok', 'k', 'v', 'o'})
        abr, abi = s5t[4], s5t[5]
        tt('dve', abr[:], rdec[:], cosT[:, :, 1], ALU.mult, [rdec, cosT], [abr])
        tt('dve', abi[:], rdec[:], sinT[:, :, 1], ALU.mult, [rdec, sinT], [abi])
        for ct in range(8):
            sdma(B.vre[:L, :, :].rearrange("p a b -> p (a b)"), s_re.t[:, ct * 512:(ct + 1) * 512], writes=[B.vre], owner=B.vre)
            sdma(B.vim[:L, :, :].rearrange("p a b -> p (a b)"), s_im.t[:, ct * 512:(ct + 1) * 512], writes=[B.vim], owner=B.vim)
            p0r, p0i, pbr, pbi = nps(), nps(), nps(), nps()
            for i in range(4):
                t = ct * 4 + i
                P.op('pe', lambda e, i=i: e.transpose(p0r[:, i * 16:(i + 1) * 16], B.vre[:L, i, :], ident[:L, :L]), reads=[B.vre, ident], writes=[p0r])
                P.op('pe', lambda e, i=i: e.transpose(p0i[:, i * 16:(i + 1) * 16], B.vim[:L, i, :], ident[:L, :L]), reads=[B.vim, ident], writes=[p0i])
                mm(pbr[:, i * 16:(i + 1) * 16], BTre[:, t, :], B.uT[:, ct, 0:L], True, True, [BTre, B.uT], pbr)
                mm(pbi[:, i * 16:(i + 1) * 16], BTim[:, t, :], B.uT[:, ct, 0:L], True, True, [BTim, B.uT], pbi)
            for i in range(4):
                t = ct * 4 + i
                sl = slice(i * 16, (i + 1) * 16)
                ts('dve', B.tA[:, i, :L], p0i[:, sl], abi[:, t:t + 1], None, ALU.mult, None, [p0i, abi], [(B.tA, i)])
                stt('dve', B.gre[:, i, :L], p0r[:, sl], abr[:, t:t + 1], B.tA[:, i, :L], ALU.mult, ALU.subtract, [p0r, abr, (B.tA, i)], [(B.gre, i)])
                tt('dve', B.gre[:, i, :L], B.gre[:, i, :L], pbr[:, sl], ALU.add, [(B.gre, i), pbr], [(B.gre, i)])
                ts('dve', B.tB[:, i, :L], p0r[:, sl], abi[:, t:t + 1], None, ALU.mult, None, [p0r, abi], [(B.tB, i)])
                stt('dve', B.gim[:, i, :L], p0i[:, sl], abr[:, t:t + 1], B.tB[:, i, :L], ALU.mult, ALU.add, [p0i, abr, (B.tB, i)], [(B.gim, i)])
                tt('dve', B.gim[:, i, :L], B.gim[:, i, :L], pbi[:, sl], ALU.add, [(B.gim, i), pbi], [(B.gim, i)])
            cp('pool', B.pr[0][:, :, :L], B.gre[:, :, :L], [B.gre], [B.pr[0]])
            cp('pool', B.pr[1][:, :, :L], B.gim[:, :, :L], [B.gim], [B.pr[1]])
            psy = nps()
            n = 0
            for i in range(4):
                t = ct * 4 + i
                for j, CT in enumerate((CTre, CTnim)):
                    mm(psy[:, :L], CT[:, t, :], B.pr[j][:, i, :L], n == 0, n == 7, [CT, B.pr[j]], psy)
                    n += 1
            yv = B.ysk[:, :L]
            stt('dve', yv, B.uT[:, ct, 0:L], dsk[:, ct:ct + 1], psy[:, :L], ALU.mult, ALU.add, [B.uT, dsk, psy], [B.ysk])
            t1 = B.tA[:, 0, :L]
            t2 = B.tB[:, 0, :L]
            tt('pool', t1, yv, yv, ALU.mult, [B.ysk], [B.tA])
            ts('pool', t1, t1, 0.044715, 1.0, ALU.mult, ALU.add, [B.tA], [B.tA])
            tt('pool', t1, t1, yv, ALU.mult, [B.tA, B.ysk], [B.tA])
            act(t2, t1, AF.Sigmoid, [B.tA], [B.tB], scale=1.5957691216057308)
            tt('dve', gT[:, ct, 0:L], yv, t2, ALU.mult, [B.ysk, B.tB], [gT])
            por, poi = nps(), nps()
            for i in range(4):
                P.op('pe', lambda e, i=i: e.transpose(por[:L, i * 128:(i + 1) * 128], B.gre[:, i, :L], ident[:, :]), reads=[B.gre, ident], writes=[por])
                P.op('pe', lambda e, i=i: e.transpose(poi[:L, i * 128:(i + 1) * 128], B.gim[:, i, :L], ident[:, :]), reads=[B.gim, ident], writes=[poi])
            act(B.vre[:L, :, :].rearrange("p a b -> p (a b)"), por[:L, :512], AF.Copy, [por], [B.vre])
            act(B.vim[:L, :, :].rearrange("p a b -> p (a b)"), poi[:L, :512], AF.Copy, [poi], [B.vim])
            sdma(o_sre.t[:, ct * 512:(ct + 1) * 512], B.vre[:L, :, :].rearrange("p a b -> p (a b)"), reads=[B.vre], owner=B.vre)
            sdma(o_sim.t[:, ct * 512:(ct + 1) * 512], B.vim[:L, :, :].rearrange("p a b -> p (a b)"), reads=[B.vim], owner=B.vim)
        m0 = B.negM
        sdma(m0[:, :L], s_mT.t[:, :], writes=[B.negM], owner=B.negM)
        act(B.fpr[:, :L], B.fpr[:, :L], AF.Exp, [B.fpr], [B.fpr], scale=-1.0)
        act(B.fpr[:, :L], B.fpr[:, :L], AF.Ln, [B.fpr], [B.fpr], bias=1.0)
        tt('dve', B.Arow[:, :L], B.igr[:, :L], B.fpr[:, :L], ALU.add, [B.igr, B.fpr], [B.Arow])
        tt('dve', B.Frow[:, :L], m0[:, :L], B.Arow[:, :L], ALU.max, [B.negM, B.Arow], [B.Frow])
        tt('dve', B.R4[:, 0, :L], m0[:, :L], B.Frow[:, :L], ALU.subtract, [B.negM, B.Frow], [(B.R4, 0)])
        tt('dve', B.R4[:, 1, :L], B.Arow[:, :L], B.Frow[:, :L], ALU.subtract, [B.Arow, B.Frow], [(B.R4, 1)])
        tt('dve', B.R4[:, 2, :L], B.fpr[:, :L], B.Frow[:, :L], ALU.subtract, [B.fpr, B.Frow], [(B.R4, 2)])
        for r in range(3):
            act(B.R4[:, r, :L], B.R4[:, r, :L], AF.Exp, [(B.R4, r)], [(B.R4, r)])
        tt('dve', B.R4[:, 3, :L], B.Frow[:, :L], B.fpr[:, :L], ALU.subtract, [B.Frow, B.fpr], [(B.R4, 3)])
        sdma(o_smT.t[:, :], B.R4[:, 3, :L], reads=[(B.R4, 3)], owner=B.R4)
        pst = nps()
        for r in range(3):
            P.op('pe', lambda e, r=r: e.transpose(pst[:L, r * 4:(r + 1) * 4], B.R4[:, r, :L], ident[:4, :4]), reads=[(B.R4, r), ident], writes=[pst])
        cp('dve', colsb[:L, 0:12], pst[:L, :12], [pst], [colsb])
        psc = nps()
        for h in range(H):
            P.op('pe', lambda e, h=h: e.matmul(psc[:, h * 16:(h + 1) * 16], sel[:, h * 128:(h + 1) * 128], B.R4[:, 0, :L], start=True, stop=True),
                 reads=[sel, (B.R4, 0)], writes=[psc])
        cp('dve', B.carbS[:, 0:64], psc[:, 0:64], [psc], [B.carbS])
        sdma(B.nst[:L, 0:1024], s_n.t[:, :], writes=[B.nst], owner=B.nst)
        Cst = [P.sbuf("Cst%d" % i, [128, 2, 256], F32) for i in range(3)]
        Cou = [P.sbuf("Cou%d" % i, [128, 2, 256], F32) for i in range(2)]
        Qm = P.sbuf("Qm", [128, 2, 16, 16], BF16)
        kwb = P.sbuf("kwb", [16, 256], BF16)
        for h in range(H):
            hs = slice(h * 256, (h + 1) * 256)
            qh = B.qtk[:L, hs]
            tt('dve', B.hh[:L, :], qh, B.ktok[:L, 0, hs], ALU.mult, [B.qtk, (B.ktok, 0)], [B.hh])
            act(B.hh[:L, :], B.hh[:L, :], AF.Copy, [B.hh], [B.hh, (sm, 12)], accum_out=sm[:L, 12:13])
            tt('dve', B.hh[:L, :], qh, B.nst[:L, hs], ALU.mult, [B.qtk, B.nst], [B.hh])
            act(B.hh[:L, :], B.hh[:L, :], AF.Copy, [B.hh], [B.hh, (sm, 13)], accum_out=sm[:L, 13:14])
            tt('dve', sm[:L, 14:15], sm[:L, 12:13], colsb[:L, 4 + h:5 + h], ALU.mult, [(sm, 12), colsb], [(sm, 14)])
            stt('dve', B.nd[:L, 256:257], sm[:L, 13:14], colsb[:L, h:h + 1], sm[:L, 14:15], ALU.mult, ALU.add,
                [(sm, 13), colsb, (sm, 14)], [B.nd])
            ts('dve', B.kwt[:L, :], B.ktok[:L, 0, hs], colsb[:L, 4 + h:5 + h], None, ALU.mult, None, [(B.ktok, 0), colsb], [B.kwt])
            stt('dve', B.nst[:L, hs], B.nst[:L, hs], colsb[:L, h:h + 1], B.kwt[:L, :],
                ALU.mult, ALU.add, [B.nst, colsb, B.kwt], [B.nst])
            for c in range(2):
                qa = B.qT[:, 2 * h + c, 0:L]
                qbc = bass.AP(qa.tensor, qa.offset, [list(qa.ap[0]), [0, L], list(qa.ap[1])])
                tt('dve', Qm[:, c, :, :], qbc, emask[:, :].rearrange("p (a b) -> p a b", a=L), ALU.mult, [B.qT, emask], [(Qm, c)])
            psE = PSX
            for b in range(L):
                cs_ = Cst[(h * L + b) % 3]
                co_ = Cou[(h * L + b) % 2]
                P.dma('pool', cs_[:, :, :], s_c.t[b, h].rearrange("(c p) v -> p c v", p=128), writes=[cs_], owner=cs_)
                act(B.Cbb[:, :, :], cs_[:, :, :], AF.Copy, [cs_], [B.Cbb])
                for c in range(2):
                    mm(psE[:L, :256], Qm[:, c, b, :], B.Cbb[:, c, :],
                       (b == 0 and c == 0), (b == L - 1 and c == 1), [(Qm, c), B.Cbb], psE)
                ts('dve', kwb[:L, :], B.kwt[:L, :], ident[:L, b:b + 1], None, ALU.mult, None, [B.kwt, ident], [kwb])
                psU = nps()
                for c in range(2):
                    mm(psU[:, c * 256:(c + 1) * 256], kwb[:L, c * 128:(c + 1) * 128], B.v1[:L, 0, h, 0:256], True, True, [kwb, (B.v1, 0)], psU)
                stt('dve', co_[:, :, :].rearrange("p a b -> p (a b)"), cs_[:, :, :].rearrange("p a b -> p (a b)"),
                    B.carbS[:, h * 16 + b:h * 16 + b + 1], psU[:, :512], ALU.mult, ALU.add, [cs_, B.carbS, psU], [co_])
                sdma(o_sc.t[b, h].rearrange("(c p) v -> p c v", p=128), co_[:, :, :], reads=[co_], owner=co_)
            ts('dve', B.tmpi[:L, 0:256], B.v1[:L, 0, h, 0:256], sm[:L, 14:15], None, ALU.mult, None, [(B.v1, 0), (sm, 14)], [B.tmpi])
            stt('dve', B.nd[:L, 0:256], psE[:L, :256], colsb[:L, h:h + 1], B.tmpi[:L, 0:256], ALU.mult, ALU.add, [psE, colsb, B.tmpi], [B.nd])
            mlstm_out(h, 0, L)
        sdma(o_sn.t[:, :], B.nst[:L, 0:1024], reads=[B.nst], owner=B.nst)
        close_scope()
        tail(tiles, L, 128, gelu_pending=False)

    gfi = [0]

    def run_prompt_block(tiles, full, mask_off=None, fill=()):
        ntok = sum(t[0] for t in tiles)
        tb = max(128, ntok)
        alloc_A(tb, state_only=not full)
        if not full:
            WB.append(P.sbuf("wb3_%d" % tagc[0], [128, 16, 512], BF16))
        load_x(tiles)
        make_xT(xtok, tiles, xT)
        in_proj(tiles, ntok, {'u', 'q', 'kF', 'k', 'v', 'o'} if full else {'u', 'k', 'v'}, mask_off)
        gate_rows(ntok)
        def do_fill(n):
            for _ in range(n):
                if fill and gfi[0] < len(fill):
                    load_w(*fill[gfi[0]])
                    gfi[0] += 1
        per = 1 if fill else 0
        itn = [0]
        off = 0
        for i, (L, _, _) in enumerate(tiles):
            g1 = s5_chunk(off, L, full)
            g2 = mlstm_chunk(i, off, L, full)
            live1, live2 = True, True
            while live1 or live2:
                for _ in range(2):
                    if live1:
                        live1 = next(g1, 'done') != 'done'
                if live2:
                    live2 = next(g2, 'done') != 'done'
                itn[0] += 1
                if full or itn[0] % 5 < 4:
                    do_fill(per)
            off += L
        gate_rows_end(ntok)
        if not full:
            WB.pop()
        close_scope()
        if full:
            tail(tiles, ntok, tb)
        return tiles[-1][0]

    npre_full = (NPRE - NMETA) // TB
    fills = []
    for c0 in (C_Q, C_O, C_GA, C_GA + 1024, C_GB, C_GB + 1024):
        fills += [(w_in, 0, 16, c0, 512), (w_in, 0, 16, c0 + 512, 512)]
    fills += [(w_glu, 0, 8, g * 512, 512) for g in range(2)]
    fills += [(w_aup, 0, 8, g * 512, 512) for g in range(4)] + [(w_bup, 0, 8, g * 512, 512) for g in range(4)]
    fills += [(w_out, 0, 16, g * 512, 512) for g in range(4)]
    for (g0, ng) in ((0, 3), (3, 3), (6, 3), (9, 2)):
        for gg in range(ng):
            fills += [(w_gate, 0, 16, (g0 + gg) * 512, 512), (w_up, 0, 16, (g0 + gg) * 512, 512)]
        fills += [(w_down, g0 * 512, ng * 4, cg * 512, 512) for cg in range(4)]
    for b in range(npre_full):
        tiles = [(128, x_pre.t[b * TB + i * 128: b * TB + (i + 1) * 128, :], None) for i in range(NCH)]
        run_prompt_block(tiles, False, b * TB, fills)
    run_prompt_block([(NMETA, x_pre.t[NPRE - NMETA:NPRE, :], None)], False, NPRE - NMETA)
    if debug == 'lead':
        dump(hst, hst[:, 0, :], 32)
        dump(Cn, Cn[:, 0, 0, :], 257, c0=64)
        P.finish('sp')
        return
    lastL = NMETA
    nblocks = NB if debug is None else 1
    for b in range(nblocks):
        tiles = [(128, x_p.t[b * TB + i * 128: b * TB + (i + 1) * 128, :], y_p.t[b * TB + i * 128: b * TB + (i + 1) * 128, :])
                 for i in range(NCH)]
        lastL = run_prompt_block(tiles, True)

    s5_state_out(lastL, o_pre, o_pim)
    for h in range(H):
        for c in range(2):
            sdma(o_pc.t[h, c * 128:(c + 1) * 128, :], Cn[:, h, c, 0:256], reads=[(Cn, h)], writes=[], owner=Cn)
    with nc.allow_non_contiguous_dma(reason="small state column"):
        sdma(o_pn.t[:, :], Cn[:, :, :, 256], reads=[Cn], writes=[], owner=Cn)
    tt('dve', carr[:, 1:2], Fc[:, 0:1], Mext[:, 0:1], ALU.add, [Fc, Mext], [carr])
    sdma(o_pm.t[:, :], carr[:, 1:2], reads=[carr], writes=[], owner=carr)
    if debug is None or debug == 'samp':
        run_sample_block()

    P.finish('sp')


def _consts():
    c = {}
    c["c_id"] = np.eye(128, dtype=np.float32)
    s = np.arange(128)
    c["c_nm"] = np.where(s[:, None] <= s[None, :], 0.0, NEG).astype(np.float32)
    sel = np.zeros((4, 4, 128), np.float32)
    for h in range(4):
        sel[h, h, :] = 1.0
    c["c_sel"] = sel.reshape(4, 512)
    c["c_j"] = np.broadcast_to(np.arange(130, dtype=np.float32), (128, 130)).copy()
    em = np.zeros((128, NSAMP, NSAMP), np.float32)
    for b in range(NSAMP):
        em[:, b, b] = 1.0
    c["c_em"] = em.reshape(128, NSAMP * NSAMP)
    return c


def _lay_gp(a):
    return np.ascontiguousarray(a.reshape(32, 128).T)


def _pad_b(b):
    out = np.zeros((128, 32, 128), np.float32)
    for g in range(G):
        t, g2 = g // 2, g % 2
        c0 = 32 * (t % 4) + 16 * g2
        out[g2 * 64:(g2 + 1) * 64, t, c0:c0 + 16] = b[g]
    return out


def _pad_c(c):
    out = np.zeros((128, 32, 128), np.float32)
    for g in range(G):
        t, g2 = g // 2, g % 2
        r0 = 32 * (t % 4) + 16 * g2
        out[r0:r0 + 16, t, g2 * 64:(g2 + 1) * 64] = c[g]
    return out


def _chan(v):
    return np.ascontiguousarray(v.reshape(8, 128).T)


_NC_CACHE = {}


def kernel(x_prompt, x_sample, state_ssm_re, state_ssm_im, state_mlstm_c, state_mlstm_n, state_mlstm_m,
           meta_tokens, w_in, b_if, ssm_a_re, ssm_a_im, ssm_log_dt, ssm_b_re, ssm_b_im, ssm_c_re, ssm_c_im,
           ssm_d, w_glu, b_glu, w_a_up, mh_gain, w_b_up, w_out, ln1_g, ln1_b, w_gate, w_up, w_down,
           ln2_g, ln2_b, _debug=None):
    f = lambda a: np.ascontiguousarray(np.asarray(a, dtype=np.float32))
    shared = {
        "w_in": f(w_in[0]),
        "b_if": f(np.asarray(b_if[0]).reshape(2, 4).T),
        "a_re": _lay_gp(f(ssm_a_re[0])), "a_im": _lay_gp(f(ssm_a_im[0])), "l_dt": _lay_gp(f(ssm_log_dt[0])),
        "bp_re": _pad_b(f(ssm_b_re[0])), "bp_im": _pad_b(f(ssm_b_im[0])),
        "cp_re": _pad_c(f(ssm_c_re[0])), "cp_im": _pad_c(f(ssm_c_im[0])),
        "d_sk": _chan(f(ssm_d[0])), "w_glu": f(w_glu[0]), "b_glu": _chan(f(b_glu[0])),
        "w_aup": f(w_a_up[0]), "mhg": _chan(f(mh_gain[0])), "w_bup": f(w_b_up[0]), "w_out": f(w_out[0]),
        "ln_gb": f(np.stack([ln1_g[0], ln1_b[0], ln2_g[0], ln2_b[0]])),
        "w_gate": f(w_gate[0]), "w_up": f(w_up[0]), "w_down": f(w_down[0]),
    }
    shared.update(_consts())
    xp = f(x_prompt)
    meta_f = f(meta_tokens)
    xs = f(x_sample).reshape(128, D)
    sre = f(state_ssm_re[0]).reshape(128, G * PST)
    sim = f(state_ssm_im[0]).reshape(128, G * PST)
    sc = f(state_mlstm_c[0])
    sn = f(state_mlstm_n[0]).reshape(128, H * DK)
    smm = f(state_mlstm_m[0])
    in_maps = []
    for c in range(8):
        m = dict(shared)
        sq, half = c // 2, c % 2
        m["x_p"] = np.ascontiguousarray(xp[sq, half * NMAIN:(half + 1) * NMAIN])
        gm = np.zeros((2, 4, NPRE), np.float32)
        if half == 0:
            m["x_pre"] = np.ascontiguousarray(np.concatenate([np.zeros((NMAIN, D), np.float32), meta_f]))
            gm[0, :, :NMAIN] = NEG
            gm[1, :, :NMAIN] = -NEG
        else:
            m["x_pre"] = np.ascontiguousarray(np.concatenate([meta_f, xp[sq, :NMAIN]]))
        m["g_mk"] = gm
        sl = slice(c * NSAMP, (c + 1) * NSAMP)
        m["x_s"] = xs[sl]
        m["s_re"] = sre[sl]
        m["s_im"] = sim[sl]
        m["s_c"] = sc[sl]
        m["s_n"] = sn[sl]
        m["s_mT"] = np.ascontiguousarray(smm[sl].T)
        in_maps.append(m)
    key = _debug
    if key not in _NC_CACHE:
        _NC_CACHE[key] = build(_debug)
    nc = _NC_CACHE[key]
    res = run_bass_kernel_spmd(nc, in_maps, core_ids=list(range(8)))
    R = res.results
    if _debug:
        return R
    y_prompt = np.stack([np.concatenate([R[2 * c]["y_p"], R[2 * c + 1]["y_p"]]) for c in range(4)])
    y_sample = np.concatenate([R[c]["y_s"] for c in range(8)]).reshape(128, 1, D)

    def gp(a):
        return np.ascontiguousarray(a.T).reshape(G, PST)
    p_re = np.stack([gp(R[2 * c + 1]["o_pre"]) for c in range(4)])[None]
    p_im = np.stack([gp(R[2 * c + 1]["o_pim"]) for c in range(4)])[None]
    p_c = np.stack([R[2 * c + 1]["o_pc"] for c in range(4)])[None]
    p_n = np.stack([R[2 * c + 1]["o_pn"].reshape(128, H, 2).transpose(1, 2, 0).reshape(H, DK) for c in range(4)])[None]
    p_m = np.stack([R[2 * c + 1]["o_pm"].reshape(H) for c in range(4)])[None]
    s_re_o = np.concatenate([R[c]["o_sre"] for c in range(8)]).reshape(1, 128, G, PST)
    s_im_o = np.concatenate([R[c]["o_sim"] for c in range(8)]).reshape(1, 128, G, PST)
    s_c_o = np.concatenate([R[c]["o_sc"] for c in range(8)])[None]
    s_n_o = np.concatenate([R[c]["o_sn"] for c in range(8)]).reshape(1, 128, H, DK)
    s_m_o = np.concatenate([R[c]["o_smT"].T for c in range(8)])[None]
    return (y_prompt, y_sample, p_re, p_im, p_c, p_n, p_m, s_re_o, s_im_o, s_c_o, s_n_o, s_m_o)
```

```python
import math
from contextlib import ExitStack
import numpy as np
import concourse.bass as bass
import concourse.mybir as mybir
from concourse.bass_utils import run_bass_kernel_spmd

F32 = mybir.dt.float32
BF16 = mybir.dt.bfloat16
I32 = mybir.dt.int32
AF = mybir.ActivationFunctionType
ALU = mybir.AluOpType

D = 2048
SEQ = 2048
NMETA = 16
NSAMP = 16
G, PST, SG = 64, 64, 16
H, DK, DV = 4, 256, 256
DFF = 5632
NIN = 9224
EPS = 1e-5
ALPHA = 2.0 ** 0.25
NCH = 2
NMAIN = 1024
NPRE = 1040
NEG = -30000.0
C_U, C_Q, C_K, C_V, C_O, C_IF, C_GA, C_GB = 0, 1024, 2048, 3072, 4096, 5120, 5128, 7176


class Buf:
    def __init__(self, t, name):
        self.t = t
        self.name = name
        self.acc = []
        self.dsem = None
        self.dcnt = 0

    def __getitem__(self, idx):
        return self.t[idx]

    @staticmethod
    def _conf(p, q):
        return p is None or q is None or p == q

    def deps(self, part, kind):
        out = []
        for (p, k, r) in self.acc:
            if not self._conf(p, part):
                continue
            if kind == 'r' and k == 'r':
                continue
            out.append(r)
        return out

    def record(self, part, kind, ref):
        if kind == 'w':
            self.acc = [(p, k, r) for (p, k, r) in self.acc if not self._conf(p, part)]
        else:
            self.acc = [(p, k, r) for (p, k, r) in self.acc
                        if not (k == 'r' and p == part and r[0] == ref[0])]
        self.acc.append((part, kind, ref))


class Prog:
    ENG = ('pe', 'act', 'dve', 'pool', 'sp')

    def __init__(self, nc, stack):
        self.nc = nc
        self.stack = stack
        self.semstack = stack
        self.eng = {'pe': nc.tensor, 'act': nc.scalar, 'dve': nc.vector, 'pool': nc.gpsimd, 'sp': nc.sync}
        self.sem = {}
        self.cnt = {}
        for e in self.ENG:
            self.sem[e] = stack.enter_context(nc.semaphore('s_' + e))
            self.cnt[e] = 0
        self.wm = {}
        self.dma_bufs = []
        self.n_inst = 0
        self.uid = 0

    def sbuf(self, name, shape, dt):
        return Buf(self.stack.enter_context(self.nc.sbuf_tensor(name, list(shape), dt)), name)

    def psum(self, name, shape, dt=F32):
        return Buf(self.stack.enter_context(self.nc.psum_tensor(name, list(shape), dt)), name)

    def dram(self, name, shape, dt, kind):
        return Buf(self.nc.dram_tensor(name, list(shape), dt, kind=kind), name)

    def _wait(self, e, ref):
        key, sem, val = ref[0], ref[2], ref[3]
        k = (e, key)
        if self.wm.get(k, 0) >= val:
            return
        self.wm[k] = val
        self.eng[e].wait_ge(sem, val)
        self.n_inst += 1

    @staticmethod
    def _norm(lst):
        return [(x, None) if isinstance(x, Buf) else x for x in lst]

    def _gather(self, reads, writes):
        deps = []
        for b, p in reads:
            deps += b.deps(p, 'r')
        for b, p in writes:
            deps += b.deps(p, 'w')
        return deps

    def op(self, e, fn, reads=(), writes=()):
        reads, writes = self._norm(reads), self._norm(writes)
        for r in self._gather(reads, writes):
            if e == 'pe' and r[1] == 'pe' and r[0][0] == 'c':
                continue
            self._wait(e, r)
        inst = fn(self.eng[e])
        self.cnt[e] += 1
        inst.then_inc(self.sem[e], 1)
        self.n_inst += 1
        ref = ('c' + e, e, self.sem[e], self.cnt[e])
        for b, p in reads:
            b.record(p, 'r', ref)
        for b, p in writes:
            b.record(p, 'w', ref)
        return inst

    def dma(self, q, out_ap, in_ap, reads=(), writes=(), owner=None, **kw):
        reads, writes = self._norm(reads), self._norm(writes)
        for r in self._gather(reads, writes):
            self._wait(q, r)
        part = None
        for b, p in list(writes) + list(reads):
            if b is owner:
                part = p
                break
        part = (part, 'sw' if q == 'pool' else 'hw')
        if not hasattr(owner, 'dsems'):
            owner.dsems = {}
        if part not in owner.dsems:
            self.uid += 1
            owner.dsems[part] = [self.semstack.enter_context(self.nc.semaphore('d%d' % self.uid)), 0]
            self.dma_bufs.append((owner, part))
        ent = owner.dsems[part]
        inst = self.eng[q].dma_start(out=out_ap, in_=in_ap, **kw)
        ent[1] += 16
        inst.then_inc(ent[0], 16)
        self.n_inst += 1
        ref = ('d%s/%s' % (owner.name, part), q, ent[0], ent[1])
        for b, p in reads:
            b.record(p, 'r', ref)
        for b, p in writes:
            b.record(p, 'w', ref)
        return inst

    def barrier(self):
        for e in self.ENG:
            for (b, p) in self.dma_bufs:
                ent = b.dsems[p]
                self._wait(e, ('d%s/%s' % (b.name, p), e, ent[0], ent[1]))
            for x in self.ENG:
                if x != e and self.cnt[x] > 0:
                    self._wait(e, ('c' + x, x, self.sem[x], self.cnt[x]))

    def finish(self, e='sp'):
        for (b, p) in self.dma_bufs:
            ent = b.dsems[p]
            self._wait(e, ('d%s/%s' % (b.name, p), e, ent[0], ent[1]))
        for x in self.ENG:
            if x != e and self.cnt[x] > 0:
                self._wait(e, ('c' + x, x, self.sem[x], self.cnt[x]))


class K:
    pass


def build(debug=None):
    nc = bass.Bass("TRN2", target_bir_lowering=False)
    st = ExitStack()
    with st:
        P = Prog(nc, st)
        _build(P, debug)
        print("kernel: n_inst", P.n_inst, {e: P.cnt[e] for e in P.ENG}, flush=True)
    return nc


def _build(P, debug):
    nc = P.nc
    TB = NCH * 128
    NB = NMAIN // TB
    din = lambda n, s, dt=F32: P.dram(n, s, dt, "ExternalInput")
    dout = lambda n, s, dt=F32: P.dram(n, s, dt, "ExternalOutput")

    x_p = din("x_p", [NMAIN, D])
    x_pre = din("x_pre", [NPRE, D])
    g_mk = din("g_mk", [2, 4, NPRE])
    x_s = din("x_s", [NSAMP, D])
    s_re = din("s_re", [NSAMP, G * PST])
    s_im = din("s_im", [NSAMP, G * PST])
    s_c = din("s_c", [NSAMP, H, DK, DV])
    s_n = din("s_n", [NSAMP, H * DK])
    s_mT = din("s_mT", [H, NSAMP])
    w_in = din("w_in", [D, NIN])
    b_if = din("b_if", [H, 2])
    a_re = din("a_re", [128, 32])
    a_im = din("a_im", [128, 32])
    l_dt = din("l_dt", [128, 32])
    bp_re = din("bp_re", [128, 32, 128])
    bp_im = din("bp_im", [128, 32, 128])
    cp_re = din("cp_re", [128, 32, 128])
    cp_im = din("cp_im", [128, 32, 128])
    d_sk = din("d_sk", [128, 8])
    w_glu = din("w_glu", [1024, 1024])
    b_glu = din("b_glu", [128, 8])
    w_aup = din("w_aup", [1024, D])
    mhg = din("mhg", [128, 8])
    w_bup = din("w_bup", [1024, D])
    w_out = din("w_out", [D, D])
    ln_gb = din("ln_gb", [4, D])
    w_gate = din("w_gate", [D, DFF])
    w_up = din("w_up", [D, DFF])
    w_down = din("w_down", [DFF, D])
    c_id = din("c_id", [128, 128])
    c_nm = din("c_nm", [128, 128])
    c_sel = din("c_sel", [4, 4 * 128])
    c_j = din("c_j", [128, 130])
    c_em = din("c_em", [128, NSAMP * NSAMP])

    y_p = dout("y_p", [NMAIN, D])
    y_s = dout("y_s", [NSAMP, D])
    o_pre = dout("o_pre", [128, 32])
    o_pim = dout("o_pim", [128, 32])
    o_pc = dout("o_pc", [H, DK, DV])
    o_pn = dout("o_pn", [128, H * 2])
    o_pm = dout("o_pm", [H, 1])
    o_sre = dout("o_sre", [NSAMP, G * PST])
    o_sim = dout("o_sim", [NSAMP, G * PST])
    o_sc = dout("o_sc", [NSAMP, H, DK, DV])
    o_sn = dout("o_sn", [NSAMP, H * DK])
    o_smT = dout("o_smT", [H, NSAMP])
    dbg = dout("dbg", [128, 4096]) if debug else None

    PS = [P.psum("ps%d" % i, [128, 512]) for i in range(6)]
    PSX = P.psum("psx", [128, 512])
    PSB = P.psum("psb", [128, 1024], BF16)
    psi = [0]

    def nps():
        psi[0] = (psi[0] + 1) % len(PS)
        return PS[psi[0]]

    WB = [P.sbuf("wb%d" % i, [128, 16, 512], BF16) for i in range(2)]
    wbi = [0]

    ident = P.sbuf("ident", [128, 128], F32)
    identb = P.sbuf("identb", [128, 128], BF16)
    negmask = P.sbuf("negmask", [128, 128], F32)
    sel = P.sbuf("sel", [4, 4 * 128], F32)
    ones4 = P.sbuf("ones4", [4, 128 * NCH], F32)
    emask = P.sbuf("emask", [128, NSAMP * NSAMP], F32)
    cosT = P.sbuf("cosT", [128, 32, 130], F32)
    sinT = P.sbuf("sinT", [128, 32, 130], F32)
    rdec = P.sbuf("rdec", [128, 32], F32)
    BTre = P.sbuf("BTre", [128, 32, 128], BF16)
    BTim = P.sbuf("BTim", [128, 32, 128], BF16)
    CTre = P.sbuf("CTre", [128, 32, 128], BF16)
    CTnim = P.sbuf("CTnim", [128, 32, 128], BF16)
    dsk = P.sbuf("dsk", [128, 8], F32)
    bglu = P.sbuf("bglu", [128, 8], F32)
    mhgs = P.sbuf("mhgs", [128, 8], F32)
    bif = P.sbuf("bif", [4, 2], F32)
    wif = P.sbuf("wif", [128, 16, 8], BF16)
    hst = P.sbuf("hst", [128, 2, 32], F32)
    Cn = P.sbuf("Cn", [128, H, 2, 257], F32)
    Cnb = P.sbuf("Cnb", [128, H, 2, 257], BF16)
    Fc = P.sbuf("Fc", [4, 1], F32)
    Mext = P.sbuf("Mext", [4, TB + 1], F32)
    negMp = P.sbuf("negMp", [128, H], F32)

    wcache = {}

    def load_w(wd, r0, kc, c0, ncols, buf=None):
        if buf is None:
            wbi[0] = (wbi[0] + 1) % len(WB)
            b = WB[wbi[0]]
        else:
            b = buf
        key = (wd.name, r0, kc, c0, ncols)
        if key in wcache:
            scr = wcache[key]
            P.dma('sp', b[:, :kc, :ncols], scr.t[:, :].rearrange("p (k n) -> p k n", k=kc), reads=[scr], writes=[b], owner=b)
        else:
            src = wd.t[r0:r0 + 128 * kc, c0:c0 + ncols].rearrange("(k p) n -> p k n", p=128)
            P.dma('pool', b[:, :kc, :ncols], src, writes=[b], owner=b)
            scr = P.dram("wc%d" % len(wcache), [128, kc * ncols], BF16, "Internal")
            wcache[key] = scr
            P.dma('sp', scr.t[:, :].rearrange("p (k n) -> p k n", k=kc), b[:, :kc, :ncols], reads=[b], writes=[scr], owner=b)
        return b

    def mm(ps_ap, lhsT, rhs, start, stop, reads, psbuf):
        P.op('pe', lambda e: e.matmul(ps_ap, lhsT, rhs, start=start, stop=stop), reads=reads, writes=[psbuf])

    def act(out_ap, in_ap, func, reads, writes, **kw):
        P.op('act', lambda e: e.activation(out_ap, in_ap, func, **kw), reads=reads, writes=writes)

    def tt(eng, out_ap, a, b, op, reads, writes):
        P.op(eng, lambda e: e.tensor_tensor(out_ap, a, b, op), reads=reads, writes=writes)

    def ts(eng, out_ap, a, s1, s2, op0, op1, reads, writes):
        if op1 is None:
            P.op(eng, lambda e: e.tensor_scalar(out_ap, a, s1, None, op0), reads=reads, writes=writes)
        else:
            P.op(eng, lambda e: e.tensor_scalar(out_ap, a, s1, s2, op0, op1), reads=reads, writes=writes)

    def stt(eng, out_ap, a, s, b, op0, op1, reads, writes):
        P.op(eng, lambda e: e.scalar_tensor_tensor(out_ap, a, s, b, op0, op1), reads=reads, writes=writes)

    def cp(eng, out_ap, in_ap, reads, writes):
        P.op(eng, lambda e: e.tensor_copy(out_ap, in_ap), reads=reads, writes=writes)

    def memset(eng, buf, ap, val):
        P.op(eng, lambda e: e.memset(ap, val), writes=[buf])

    def sdma(out_ap, in_ap, reads=(), writes=(), owner=None, **kw):
        P.dma('sp', out_ap, in_ap, reads=reads, writes=writes, owner=owner, **kw)

    def dump(buf, ap, ncol, np_=128, c0=0):
        sdma(dbg.t[:np_, c0:c0 + ncol], ap, reads=[buf], writes=[dbg], owner=buf)

    sdma(ident[:], c_id.t[:], writes=[ident], owner=ident)
    sdma(negmask[:], c_nm.t[:], writes=[negmask], owner=negmask)
    sdma(sel[:], c_sel.t[:], writes=[sel], owner=sel)
    sdma(emask[:], c_em.t[:], writes=[emask], owner=emask)
    sdma(dsk[:], d_sk.t[:], writes=[dsk], owner=dsk)
    sdma(bglu[:], b_glu.t[:], writes=[bglu], owner=bglu)
    sdma(mhgs[:], mhg.t[:], writes=[mhgs], owner=mhgs)
    sdma(bif[:], b_if.t[:], writes=[bif], owner=bif)
    cp('dve', identb[:], ident[:], [ident], [identb])
    memset('dve', ones4, ones4[:], 1.0)
    epsc = P.sbuf("epsc", [128, 1], F32)
    memset('dve', epsc, epsc[:], EPS)
    wsrc = w_in.t[:, C_IF:C_IF + 8].rearrange("(k p) n -> p k n", p=128)
    with nc.allow_non_contiguous_dma(reason="tiny gate weight columns"):
        P.dma('pool', wif[:], wsrc, writes=[wif], owner=wif)

    st2 = ExitStack()
    P.stack = st2
    s5t = [P.sbuf("s5t%d" % i, [128, 32], F32) for i in range(6)]
    are = P.sbuf("are", [128, 32], F32)
    aim = P.sbuf("aim", [128, 32], F32)
    ldt = P.sbuf("ldt", [128, 32], F32)
    th = P.sbuf("th", [128, 32], F32)
    jidx = P.sbuf("jidx", [128, 130], F32)
    sdma(are[:], a_re.t[:], writes=[are], owner=are)
    sdma(aim[:], a_im.t[:], writes=[aim], owner=aim)
    sdma(ldt[:], l_dt.t[:], writes=[ldt], owner=ldt)
    sdma(jidx[:], c_j.t[:], writes=[jidx], owner=jidx)
    act(ldt[:], ldt[:], AF.Exp, [ldt], [ldt])
    tt('dve', th[:], aim[:], ldt[:], ALU.mult, [aim, ldt], [th])
    tt('dve', rdec[:], are[:], ldt[:], ALU.mult, [are, ldt], [rdec])
    act(rdec[:], rdec[:], AF.Exp, [rdec], [rdec])
    st3 = ExitStack()
    P.stack = st3
    angb = P.sbuf("angb", [128, 32, 130], F32)
    kfi = P.sbuf("kfi", [128, 32, 130], I32)
    kfb = P.sbuf("kfb", [128, 32, 130], F32)
    for t in range(32):
        ts('dve', angb[:, t, :], jidx[:], th[:, t:t + 1], None, ALU.mult, None, [jidx, th], [(angb, t)])
    TWO_PI = 2.0 * math.pi
    C1 = 6.28125
    C2 = TWO_PI - C1

    def reduce_and_sin(dst, shift):
        ts('dve', kfb[:], angb[:], shift, 1.0 / TWO_PI, ALU.add, ALU.mult, [angb], [kfb])
        cp('dve', kfi[:], kfb[:], [kfb], [kfi])
        cp('dve', kfb[:], kfi[:], [kfi], [kfb])
        tmp = dst
        ts('dve', tmp[:], angb[:], shift, None, ALU.add, None, [angb], [tmp])
        stt('dve', tmp[:], kfb[:], -C1, tmp[:], ALU.mult, ALU.add, [kfb, tmp], [tmp])
        stt('dve', tmp[:], kfb[:], -C2, tmp[:], ALU.mult, ALU.add, [kfb, tmp], [tmp])
        ts('dve', kfb[:], tmp[:], math.pi, TWO_PI, ALU.is_gt, ALU.mult, [tmp], [kfb])
        tt('dve', tmp[:], tmp[:], kfb[:], ALU.subtract, [tmp, kfb], [tmp])
        ts('dve', kfb[:], tmp[:], -math.pi, TWO_PI, ALU.is_lt, ALU.mult, [tmp], [kfb])
        tt('dve', tmp[:], tmp[:], kfb[:], ALU.add, [tmp, kfb], [tmp])
        ts('dve', tmp[:], tmp[:], math.pi, -math.pi, ALU.min, ALU.max, [tmp], [tmp])
        act(dst[:], tmp[:], AF.Sin, [tmp], [dst])

    reduce_and_sin(sinT, 0.0)
    reduce_and_sin(cosT, math.pi / 2.0)
    P.barrier()
    st3.close()
    P.stack = st2

    c1 = cosT[:, :, 1]
    s1 = sinT[:, :, 1]
    nre, nim, den, cre, cim, t5 = s5t
    tt('dve', nre[:], rdec[:], c1, ALU.mult, [rdec, cosT], [nre])
    ts('dve', nre[:], nre[:], -1.0, None, ALU.add, None, [nre], [nre])
    tt('dve', nim[:], rdec[:], s1, ALU.mult, [rdec, sinT], [nim])
    tt('dve', den[:], are[:], are[:], ALU.mult, [are], [den])
    tt('dve', t5[:], aim[:], aim[:], ALU.mult, [aim], [t5])
    tt('dve', den[:], den[:], t5[:], ALU.add, [den, t5], [den])
    P.op('dve', lambda e: e.reciprocal(den[:], den[:]), reads=[den], writes=[den])
    tt('dve', cre[:], nre[:], are[:], ALU.mult, [nre, are], [cre])
    tt('dve', t5[:], nim[:], aim[:], ALU.mult, [nim, aim], [t5])
    tt('dve', cre[:], cre[:], t5[:], ALU.add, [cre, t5], [cre])
    tt('dve', cre[:], cre[:], den[:], ALU.mult, [cre, den], [cre])
    tt('dve', cim[:], nim[:], are[:], ALU.mult, [nim, are], [cim])
    tt('dve', t5[:], nre[:], aim[:], ALU.mult, [nre, aim], [t5])
    tt('dve', cim[:], cim[:], t5[:], ALU.subtract, [cim, t5], [cim])
    tt('dve', cim[:], cim[:], den[:], ALU.mult, [cim, den], [cim])
    Xa = P.sbuf("Xa", [128, 32, 128], F32)
    Xb = P.sbuf("Xb", [128, 32, 128], F32)
    Xc = P.sbuf("Xc", [128, 32, 128], F32)
    Xd = P.sbuf("Xd", [128, 32, 128], F32)
    sdma(Xa[:], bp_re.t[:], writes=[Xa], owner=Xa)
    sdma(Xb[:], bp_im.t[:], writes=[Xb], owner=Xb)
    for t in range(32):
        ts('dve', Xc[:, t, :], Xa[:, t, :], cre[:, t:t + 1], None, ALU.mult, None, [(Xa, t), cre], [(Xc, t)])
        ts('dve', Xd[:, t, :], Xb[:, t, :], cre[:, t:t + 1], None, ALU.mult, None, [(Xb, t), cre], [(Xd, t)])
        ts('dve', Xb[:, t, :], Xb[:, t, :], cim[:, t:t + 1], None, ALU.mult, None, [(Xb, t), cim], [(Xb, t)])
        ts('dve', Xa[:, t, :], Xa[:, t, :], cim[:, t:t + 1], None, ALU.mult, None, [(Xa, t), cim], [(Xa, t)])
        tt('dve', Xc[:, t, :], Xc[:, t, :], Xb[:, t, :], ALU.subtract, [(Xc, t), (Xb, t)], [(Xc, t)])
        tt('dve', Xd[:, t, :], Xd[:, t, :], Xa[:, t, :], ALU.add, [(Xd, t), (Xa, t)], [(Xd, t)])

    def transpose_tiles(src, dst, scale=None):
        for t4 in range(8):
            ps = nps()
            for i in range(4):
                t = t4 * 4 + i
                P.op('pe', lambda e, t=t, i=i, ps=ps: e.transpose(ps[:, i * 128:(i + 1) * 128], src[:, t, :], ident[:]),
                     reads=[(src, t), ident], writes=[ps])
            o = dst[:, t4 * 4:(t4 + 1) * 4, :]
            pin = ps[:, :].rearrange("p (a b) -> p a b", a=4)
            if scale is None:
                act(o, pin, AF.Copy, [ps], [dst])
            else:
                act(o, pin, AF.Copy, [ps], [dst], scale=scale)

    transpose_tiles(Xc, BTre)
    transpose_tiles(Xd, BTim)
    sdma(Xa[:], cp_re.t[:], writes=[Xa], owner=Xa)
    sdma(Xb[:], cp_im.t[:], writes=[Xb], owner=Xb)
    transpose_tiles(Xa, CTre)
    transpose_tiles(Xb, CTnim, scale=-1.0)

    P.barrier()
    st2.close()
    P.stack = P.semstack
    s5t = [P.sbuf("s5u%d" % i, [128, 32], F32) for i in range(6)]
    xtok = P.sbuf("xtok", [128, NCH, D], F32)
    xT = P.sbuf("xT", [128, 16, TB], BF16)
    gT = P.sbuf("gT", [128, 8, TB], BF16)
    ybT = P.sbuf("ybT", [128, 8, TB], BF16)
    colsb = P.sbuf("colsb", [128, 16], F32)
    carr = P.sbuf("carr", [4, 8], F32)
    carb = P.sbuf("carb", [128, H], F32)
    glast = P.sbuf("glast", [128, 2, 32], F32)
    sm = P.sbuf("sm", [128, 16], F32)
    lnst = P.sbuf("lnst", [128, 4, 6], F32)
    B = K()
    scope = [None]
    tagc = [0]

    scopes = []

    def open_scope():
        scopes.append(ExitStack())
        P.stack = scopes[-1]
        tagc[0] += 1
        return "_%d" % tagc[0]

    def close_scope():
        P.barrier()
        scopes.pop().close()
        P.stack = scopes[-1] if scopes else P.semstack

    def alloc_A(tb, state_only=False, sample=False):
        t = open_scope()
        nch = max(1, tb // 128)
        B.uT = P.sbuf("uT" + t, [128, 8, tb], BF16)
        if not state_only:
            B.qT = P.sbuf("qT" + t, [128, 8, tb], BF16)
            if not sample:
                B.kT = P.sbuf("kT" + t, [128, 8, tb], BF16)
            B.sigo = P.sbuf("sigo" + t, [128, 8, tb], BF16)
        B.v1 = P.sbuf("v1" + t, [128, nch, H, 257], BF16)
        B.ktok = P.sbuf("ktok" + t, [128, nch, 1024], BF16)
        for nm in ("igr", "fpr", "Frow", "Arow", "negM", "clampa"):
            setattr(B, nm, P.sbuf(nm + t, [4, tb], F32))
        B.R4 = P.sbuf("R4" + t, [4, 4, 128], F32)
        if state_only:
            B.gmk = P.sbuf("gmk" + t, [4, 2, tb], F32)
        for nm in ("vre", "vim", "tA", "tB", "gre", "gim"):
            setattr(B, nm, P.sbuf(nm + t, [128, 4, 128], F32))
        if state_only or sample:
            B.gre2, B.gim2 = [B.gre, B.gre], [B.gim, B.gim]
        else:
            B.gre2 = [B.gre, P.sbuf("greB" + t, [128, 4, 128], F32)]
            B.gim2 = [B.gim, P.sbuf("gimB" + t, [128, 4, 128], F32)]

        B.pr = [P.sbuf("pr%d" % i + t, [128, 4, 128], BF16) for i in range(4)]
        if not state_only:
            B.ysk = P.sbuf("ysk" + t, [128, 128], F32)
            if not sample:
                B.Wsb = P.sbuf("Wsb" + t, [128, 128], F32)
                B.PTb = P.sbuf("PTb" + t, [128, 128], BF16)
            B.tmpi = P.sbuf("tmpi" + t, [128, 257], F32)
            B.nd = P.sbuf("nd" + t, [128, 257], F32)
            B.hh = P.sbuf("hh" + t, [128, 256], F32)
            B.hnb = P.sbuf("hnb" + t, [128, 256], BF16)
        B.kwt = P.sbuf("kwt" + t, [128, 256], BF16)
        memset('pool', B.v1, B.v1[:], 1.0)

    def alloc_B(tb):
        t = open_scope()
        B.yaT = P.sbuf("yaT" + t, [128, 8, tb], BF16)
        B.mixT = P.sbuf("mixT" + t, [128, 16, tb], BF16)
        B.sgate = P.sbuf("sgate" + t, [128, 4, tb], F32)
        B.sgb = P.sbuf("sgb" + t, [128, 4, tb], BF16)

    def alloc_C(tb):
        t = open_scope()
        B.hT = P.sbuf("hT" + t, [128, 12, tb], BF16)
        B.sgate = P.sbuf("sgate" + t, [128, 4, tb], F32)

    memset('dve', hst, hst[:], 0.0)
    memset('dve', Cn, Cn[:], 0.0)
    memset('dve', Cnb, Cnb[:], 0.0)
    memset('dve', Fc, Fc[:], 0.0)
    memset('dve', Mext, Mext[:], 0.0)
    memset('dve', negMp, negMp[:], 0.0)

    def load_x(tiles):
        for i, (L, src, _) in enumerate(tiles):
            sdma(xtok[:L, i, :], src, writes=[(xtok, i)], owner=xtok)

    def make_xT(src, tiles, dstT, gbi=None):
        off = 0
        for i, (L, _, _) in enumerate(tiles):
            for k4 in range(4):
                ps = nps()
                for kk in range(4):
                    k = k4 * 4 + kk
                    P.op('pe', lambda e, k=k, kk=kk, ps=ps, L=L, i=i: e.transpose(
                        ps[:, kk * 128:kk * 128 + L], src[:L, i, k * 128:(k + 1) * 128], ident[:L, :L]),
                        reads=[(src, i), ident], writes=[ps])
                pin = ps[:, :].rearrange("p (a b) -> p a b", a=4)[:, :, :L]
                o = dstT[:, k4 * 4:(k4 + 1) * 4, off:off + L]
                eng = 'act' if (k4 % 2 == 0) else 'dve'
                if eng == 'act':
                    act(o, pin, AF.Copy, [ps], [dstT])
                else:
                    cp('dve', o, pin, [ps], [dstT])
            off += L
        return off

    def projF(wb, kc, f0, M, rhsT, ntok, ps):
        for k in range(kc):
            mm(ps[:M, :ntok], wb[:, k, f0:f0 + M], rhsT[:, k, :ntok], k == 0, k == kc - 1, [wb, rhsT], ps)

    def in_proj(tiles, ntok, need, mask_off=None):
        def grp(c0, dstT=None, fkind=None, tkind=None):
            for g in range(2):
                wb = load_w(w_in, 0, 16, c0 + g * 512, 512)
                if fkind is not None:
                    for ft in range(4):
                        ps = nps()
                        projF(wb, 16, ft * 128, 128, xT, ntok, ps)
                        o = dstT[:, g * 4 + ft, :ntok]
                        if fkind == 'copy':
                            act(o, ps[:, :ntok], AF.Copy, [ps], [dstT])
                        elif fkind == 'k':
                            act(o, ps[:, :ntok], AF.Copy, [ps], [dstT], scale=1.0 / 16.0)
                        elif fkind == 'sig':
                            act(o, ps[:, :ntok], AF.Sigmoid, [ps], [dstT])
                if tkind is not None:
                    off = 0
                    for i, (L, _, _) in enumerate(tiles):
                        ps = nps()
                        for k in range(16):
                            mm(ps[:L, :512], xT[:, k, off:off + L], wb[:, k, :512], k == 0, k == 15, [wb, xT], ps)
                        if tkind == 'q':
                            act(B.qtk[:L, g * 512:(g + 1) * 512], ps[:L, :512], AF.Copy, [ps], [B.qtk])
                        elif tkind == 'v':
                            for hh_ in range(2):
                                h = g * 2 + hh_
                                cp('dve', B.v1[:L, i, h, 0:256], ps[:L, hh_ * 256:(hh_ + 1) * 256], [ps], [(B.v1, i)])
                        else:
                            act(B.ktok[:L, i, g * 512:(g + 1) * 512], ps[:L, :512], AF.Copy, [ps], [(B.ktok, i)], scale=1.0 / 16.0)
                        off += L

        if 'u' in need:
            grp(C_U, B.uT, 'copy')
        if 'q' in need or 'qtok' in need:
            grp(C_Q, B.qT if 'q' in need else None, 'copy' if 'q' in need else None, 'q' if 'qtok' in need else None)
        if 'kF' in need or 'k' in need:
            grp(C_K, B.kT if 'kF' in need else None, 'k' if 'kF' in need else None, 'k' if 'k' in need else None)
        if 'v' in need:
            grp(C_V, None, None, 'v')
        if 'o' in need:
            grp(C_O, B.sigo, 'sig')
        for j, dst in enumerate((B.igr, B.fpr)):
            ps = nps()
            for k in range(16):
                mm(ps[:4, :ntok], wif[:, k, j * 4:(j + 1) * 4], xT[:, k, :ntok], k == 0, k == 15, [wif, xT], ps)
            act(dst[:, :ntok], ps[:4, :ntok], AF.Identity, [ps], [dst], bias=bif[:, j:j + 1])
            if mask_off is not None:
                sdma(B.gmk[:, j, :ntok], g_mk.t[j, :, mask_off:mask_off + ntok], writes=[(B.gmk, j)], owner=B.gmk)
                tt('dve', dst[:, :ntok], dst[:, :ntok], B.gmk[:, j, :ntok], ALU.add, [dst, (B.gmk, j)], [dst])

    def gate_rows(ntok):
        act(B.fpr[:, :ntok], B.fpr[:, :ntok], AF.Exp, [B.fpr], [B.fpr], scale=-1.0)
        act(B.fpr[:, :ntok], B.fpr[:, :ntok], AF.Ln, [B.fpr], [B.fpr], bias=1.0)
        P.op('dve', lambda e: e.tensor_tensor_scan(B.Frow[:, :ntok], ones4[:, :ntok], B.fpr[:, :ntok], Fc[:, 0:1],
                                                   ALU.mult, ALU.subtract), reads=[ones4, B.fpr, Fc], writes=[B.Frow])
        tt('dve', B.Arow[:, :ntok], B.igr[:, :ntok], B.Frow[:, :ntok], ALU.subtract, [B.igr, B.Frow], [B.Arow])
        P.op('dve', lambda e: e.tensor_tensor_scan(Mext[:, 1:ntok + 1], B.Arow[:, :ntok], B.Arow[:, :ntok], Mext[:, 0:1],
                                                   ALU.max, ALU.max), reads=[B.Arow, (Mext, 0)], writes=[(Mext, 1)])
        ts('dve', B.negM[:, :ntok], Mext[:, 1:ntok + 1], -1.0, None, ALU.mult, None, [(Mext, 1)], [B.negM])
        stt('dve', B.clampa[:, :ntok], B.Frow[:, :ntok], -1.0, Mext[:, 1:ntok + 1], ALU.mult, ALU.subtract,
            [B.Frow, (Mext, 1)], [B.clampa])

    def gate_rows_end(ntok):
        cp('dve', Fc[:, 0:1], B.Frow[:, ntok - 1:ntok], [B.Frow], [Fc])
        cp('dve', Mext[:, 0:1], Mext[:, ntok:ntok + 1], [(Mext, 1)], [(Mext, 0)])

    def s5_chunk(off, L, full):
        def stage_a(ct):
            gre_, gim_ = B.gre2[ct % 2], B.gim2[ct % 2]
            psr = nps()
            psi_ = nps()
            for i in range(4):
                t = ct * 4 + i
                mm(psr[:, i * 128:i * 128 + L], BTre[:, t, :], B.uT[:, ct, off:off + L], True, True, [BTre, B.uT], psr)
                mm(psi_[:, i * 128:i * 128 + L], BTim[:, t, :], B.uT[:, ct, off:off + L], True, True, [BTim, B.uT], psi_)
            pr3 = psr[:, :].rearrange("p (a b) -> p a b", a=4)[:, :, :L]
            pi3 = psi_[:, :].rearrange("p (a b) -> p a b", a=4)[:, :, :L]
            cs = cosT[:, ct * 4:(ct + 1) * 4, 0:L]
            sn = sinT[:, ct * 4:(ct + 1) * 4, 0:L]
            tt('dve', B.vre[:, :, :L], pr3, cs, ALU.mult, [psr, cosT], [B.vre])
            tt('dve', B.tA[:, :, :L], pi3, sn, ALU.mult, [psi_, sinT], [B.tA])
            tt('dve', B.vim[:, :, :L], pi3, cs, ALU.mult, [psi_, cosT], [B.vim])
            tt('dve', B.tB[:, :, :L], pr3, sn, ALU.mult, [psr, sinT], [B.tB])
            tt('dve', B.vre[:, :, :L], B.vre[:, :, :L], B.tA[:, :, :L], ALU.add, [B.vre, B.tA], [B.vre])
            tt('dve', B.vim[:, :, :L], B.vim[:, :, :L], B.tB[:, :, :L], ALU.subtract, [B.vim, B.tB], [B.vim])
            for i in range(4):
                t = ct * 4 + i
                for (vv, gg, ri) in ((B.vre, gre_, 0), (B.vim, gim_, 1)):
                    P.op('dve', lambda e, vv=vv, gg=gg, ri=ri, t=t, i=i: e.tensor_tensor_scan(
                        gg[:, i, :L], rdec[:, t:t + 1].to_broadcast([128, L]), vv[:, i, :L], hst[:, ri, t:t + 1],
                        ALU.mult, ALU.add), reads=[rdec, vv, (hst, ct)], writes=[gg])
            cp('pool', glast[:, 0, ct * 4:(ct + 1) * 4], gre_[:, :, L - 1], [gre_], [(glast, ct)])
            cp('pool', glast[:, 1, ct * 4:(ct + 1) * 4], gim_[:, :, L - 1], [gim_], [(glast, ct)])

        def stage_b(ct):
            gre_, gim_ = B.gre2[ct % 2], B.gim2[ct % 2]
            cs = cosT[:, ct * 4:(ct + 1) * 4, 0:L]
            sn = sinT[:, ct * 4:(ct + 1) * 4, 0:L]
            gr = gre_[:, :, :L]
            gi = gim_[:, :, :L]
            tt('pool', B.pr[0][:, :, :L], gr, cs, ALU.mult, [gre_, cosT], [B.pr[0]])
            tt('pool', B.pr[2][:, :, :L], gr, sn, ALU.mult, [gre_, sinT], [B.pr[2]])
            tt('pool', B.pr[3][:, :, :L], gi, cs, ALU.mult, [gim_, cosT], [B.pr[3]])
            stt('dve', B.pr[1][:, :, :L], gi, -1.0, sn, ALU.mult, ALU.mult, [gim_, sinT], [B.pr[1]])

        def stage_c(ct):
            psy = nps()
            n = 0
            for i in range(4):
                t = ct * 4 + i
                for j, CT in enumerate((CTre, CTre, CTnim, CTnim)):
                    mm(psy[:, :L], CT[:, t, :], B.pr[j][:, i, :L], n == 0, n == 15, [CT, B.pr[j]], psy)
                    n += 1
            stt('dve', gT[:, ct, off:off + L], B.uT[:, ct, off:off + L], dsk[:, ct:ct + 1], psy[:, :L], ALU.mult, ALU.add,
                [B.uT, dsk, psy], [gT])

        stage_a(0)
        for ct in range(8):
            if full:
                stage_b(ct)
            if ct + 1 < 8:
                stage_a(ct + 1)
            if full:
                stage_c(ct)
            yield
        cL = cosT[:, :, L]
        sL = sinT[:, :, L]
        glr = glast[:, 0, :]
        gli = glast[:, 1, :]
        a0, a1, a2, a3 = s5t[0], s5t[1], s5t[2], s5t[3]
        tt('dve', a0[:], glr, cL, ALU.mult, [glast, cosT], [a0])
        tt('dve', a1[:], gli, sL, ALU.mult, [glast, sinT], [a1])
        tt('dve', a2[:], glr, sL, ALU.mult, [glast, sinT], [a2])
        tt('dve', a3[:], gli, cL, ALU.mult, [glast, cosT], [a3])
        tt('dve', hst[:, 0, :], a0[:], a1[:], ALU.subtract, [a0, a1], [hst])
        tt('dve', hst[:, 1, :], a2[:], a3[:], ALU.add, [a2, a3], [hst])
        yield

    def s5_state_out(L, dre, dim_):
        cL = cosT[:, :, L - 1]
        sL = sinT[:, :, L - 1]
        glr = glast[:, 0, :]
        gli = glast[:, 1, :]
        a0, a1, a2, a3, a4, a5 = s5t
        tt('dve', a0[:], glr, cL, ALU.mult, [glast, cosT], [a0])
        tt('dve', a1[:], gli, sL, ALU.mult, [glast, sinT], [a1])
        tt('dve', a2[:], glr, sL, ALU.mult, [glast, sinT], [a2])
        tt('dve', a3[:], gli, cL, ALU.mult, [glast, cosT], [a3])
        tt('dve', a4[:], a0[:], a1[:], ALU.subtract, [a0, a1], [a4])
        tt('dve', a5[:], a2[:], a3[:], ALU.add, [a2, a3], [a5])
        sdma(dre.t[:], a4[:], reads=[a4], writes=[dre], owner=a4)
        sdma(dim_.t[:], a5[:], reads=[a5], writes=[dim_], owner=a5)

    def mlstm_out(h, off, L):
        stt('dve', sm[:L, 6:7], B.nd[:L, 256:257], -1.0, B.nd[:L, 256:257], ALU.mult, ALU.max, [B.nd], [(sm, 6)])
        tt('dve', sm[:L, 0:1], sm[:L, 6:7], colsb[:L, 8 + h:9 + h], ALU.max, [(sm, 6), colsb], [(sm, 0)])
        P.op('dve', lambda e: e.reciprocal(sm[:L, 1:2], sm[:L, 0:1]), reads=[(sm, 0)], writes=[(sm, 1)])
        ts('dve', B.hh[:L, :], B.nd[:L, 0:256], sm[:L, 1:2], None, ALU.mult, None, [B.nd, (sm, 1)], [B.hh])
        P.op('dve', lambda e: e.bn_stats(lnst[:L, 0, :], B.hh[:L, :]), reads=[B.hh], writes=[lnst])
        P.op('dve', lambda e: e.bn_aggr(sm[:L, 2:4], lnst[:L, 0, :]), reads=[lnst], writes=[(sm, 2)])
        act(sm[:L, 4:5], sm[:L, 3:4], AF.Ln, [(sm, 2)], [(sm, 4)], bias=epsc[:L, 0:1])
        act(sm[:L, 5:6], sm[:L, 4:5], AF.Exp, [(sm, 4)], [(sm, 5)], scale=-0.5)
        ts('dve', B.hnb[:L, :], B.hh[:L, :], sm[:L, 2:3], sm[:L, 5:6], ALU.subtract, ALU.mult, [B.hh, (sm, 2), (sm, 5)], [B.hnb])
        for c in range(2):
            P.op('pe', lambda e, c=c: e.transpose(PSB[:, c * 128:c * 128 + L], B.hnb[:L, c * 128:(c + 1) * 128], identb[:L, :L]),
                 reads=[B.hnb, identb], writes=[PSB])
            stt('dve', ybT[:, 2 * h + c, off:off + L], PSB[:, c * 128:c * 128 + L], mhgs[:, 2 * h + c:2 * h + c + 1],
                B.sigo[:, 2 * h + c, off:off + L], ALU.mult, ALU.mult, [PSB, mhgs, B.sigo], [ybT])

    def mlstm_chunk(ci, off, L, full):
        Me = Mext[:, off:off + 1]
        ME = Mext[:, off + L:off + L + 1]
        Mpart = (Mext, 1)
        ts('dve', B.R4[:, 0, :L], Mext[:, off + 1:off + L + 1], -1.0, Me, ALU.mult, ALU.add, [Mext], [(B.R4, 0)])
        ts('dve', B.R4[:, 1, :L], B.Arow[:, off:off + L], ME, None, ALU.subtract, None, [B.Arow, Mext], [(B.R4, 1)])
        cp('dve', B.R4[:, 2, :L], B.clampa[:, off:off + L], [B.clampa], [(B.R4, 2)])
        for r in range(3):
            act(B.R4[:, r, :L], B.R4[:, r, :L], AF.Exp, [(B.R4, r)], [(B.R4, r)])
        cp('dve', B.R4[:, 3, :L], B.Arow[:, off:off + L], [B.Arow], [(B.R4, 3)])
        tt('dve', carr[:, 0:1], Me, ME, ALU.subtract, [Mext], [carr])
        act(carr[:, 0:1], carr[:, 0:1], AF.Exp, [carr], [carr])
        ts('dve', carr[:, 4:8], ident[:4, :4], carr[:, 0:1], None, ALU.mult, None, [ident, carr], [carr])
        psc = nps()
        P.op('pe', lambda e: e.matmul(psc[:, :4], ones4[:, :128], carr[:, 4:8], start=True, stop=True),
             reads=[ones4, carr], writes=[psc])
        cp('dve', carb[:, :], psc[:, :4], [psc], [carb])
        pst = nps()
        for r in range(4):
            P.op('pe', lambda e, r=r: e.transpose(pst[:L, r * 4:(r + 1) * 4], B.R4[:, r, :L], ident[:4, :4]),
                 reads=[(B.R4, r), ident], writes=[pst])
        cp('dve', colsb[:L, :], pst[:L, :16], [pst], [colsb])
        yield
        for h in range(H):
            if full:
                psR = nps()
                P.op('pe', lambda e, h=h: e.matmul(psR[:L, :L], sel[:, h * 128:h * 128 + L], B.negM[:, off:off + L],
                                                   start=True, stop=False), reads=[sel, B.negM], writes=[psR])
                P.op('pe', lambda e: e.matmul(psR[:L, :L], ident[:L, :L], negmask[:L, :L], start=False, stop=True),
                     reads=[ident, negmask], writes=[psR])
                psS = nps()
                for c in range(2):
                    mm(psS[:L, :L], B.kT[:, 2 * h + c, off:off + L], B.qT[:, 2 * h + c, off:off + L], c == 0, c == 1, [B.kT, B.qT], psS)
                act(B.Wsb[:L, :L], psR[:L, :L], AF.Exp, [psR, colsb], [B.Wsb], bias=colsb[:L, 12 + h:13 + h])
                tt('dve', B.PTb[:L, :L], psS[:L, :L], B.Wsb[:L, :L], ALU.mult, [psS, B.Wsb], [B.PTb])
                psA = nps()
                mm(psA[:L, :257], B.PTb[:L, :L], B.v1[:L, ci, h, :], True, True, [B.PTb, (B.v1, ci)], psA)
                psE = nps()
                for c in range(2):
                    mm(psE[:L, :257], B.qT[:, 2 * h + c, off:off + L], Cnb[:, h, c, :], c == 0, c == 1, [B.qT, (Cnb, h)], psE)
                act(B.tmpi[:L, :], psA[:L, :257], AF.Copy, [psA], [B.tmpi])
                stt('dve', B.nd[:L, :], psE[:L, :257], colsb[:L, h:h + 1], B.tmpi[:L, :], ALU.mult, ALU.add, [psE, colsb, B.tmpi], [B.nd])
                mlstm_out(h, off, L)
            ts('dve', B.kwt[:L, :], B.ktok[:L, ci, h * 256:(h + 1) * 256], colsb[:L, 4 + h:5 + h], None, ALU.mult, None,
               [(B.ktok, ci), colsb], [B.kwt])
            for c in range(2):
                psU = nps()
                mm(psU[:, :257], B.kwt[:L, c * 128:(c + 1) * 128], B.v1[:L, ci, h, :], True, True, [B.kwt, (B.v1, ci)], psU)
                stt('dve', Cn[:, h, c, :], Cn[:, h, c, :], carb[:, h:h + 1], psU[:, :257], ALU.mult, ALU.add,
                    [(Cn, h), carb, psU], [(Cn, h)])
                act(Cnb[:, h, c, :], Cn[:, h, c, :], AF.Copy, [(Cn, h)], [(Cnb, h)])
            yield

    def layernorm_tile(buf, i, L, gi):
        for c in range(4):
            P.op('dve', lambda e, c=c: e.bn_stats(lnst[:L, c, :], buf[:L, i, c * 512:(c + 1) * 512]),
                 reads=[(buf, i)], writes=[lnst])
        P.op('dve', lambda e: e.bn_aggr(sm[:L, 8:10], lnst[:L, :, :].rearrange("p a b -> p (a b)")), reads=[lnst], writes=[(sm, 8)])
        ts('dve', sm[:L, 10:11], sm[:L, 9:10], EPS, None, ALU.add, None, [(sm, 8)], [(sm, 10)])
        act(sm[:L, 10:11], sm[:L, 10:11], AF.Sqrt, [(sm, 10)], [(sm, 10)])
        P.op('dve', lambda e: e.reciprocal(sm[:L, 11:12], sm[:L, 10:11]), reads=[(sm, 10)], writes=[(sm, 11)])
        ts('dve', buf[:L, i, :], buf[:L, i, :], sm[:L, 8:9], sm[:L, 11:12], ALU.subtract, ALU.mult,
           [(buf, i), (sm, 8), (sm, 11)], [(buf, i)])
        eng = 'dve' if i % 2 == 0 else 'pool'
        tt(eng, buf[:L, i, :], buf[:L, i, :], B.gb[:L, 0, :], ALU.mult, [(buf, i), B.gb], [(buf, i)])
        tt(eng, buf[:L, i, :], buf[:L, i, :], B.gb[:L, 1, :], ALU.add, [(buf, i), B.gb], [(buf, i)])

    def load_gb(gi):
        for j in range(2):
            src = ln_gb.t[gi + j:gi + j + 1, :].to_broadcast([128, D])
            P.dma('pool', B.gb[:, j, :], src, writes=[B.gb], owner=B.gb)

    def tail(tiles, ntok, tb, gelu_pending=True):
        alloc_B(tb)
        if gelu_pending:
            tg = open_scope()
            B.gtmp = P.sbuf("gtmp" + tg, [128, 8, tb], F32)
            yv = gT[:, :, :ntok]
            tm = B.gtmp[:, :, :ntok]
            tt('dve', tm, yv, yv, ALU.mult, [gT], [B.gtmp])
            ts('dve', tm, tm, 0.044715, 1.0, ALU.mult, ALU.add, [B.gtmp], [B.gtmp])
            tt('pool', tm, tm, yv, ALU.mult, [B.gtmp, gT], [B.gtmp])
            act(tm, tm, AF.Sigmoid, [B.gtmp], [B.gtmp], scale=1.5957691216057308)
            tt('dve', yv, yv, tm, ALU.mult, [gT, B.gtmp], [gT])
            close_scope()
        tw = open_scope()
        WB.append(P.sbuf("wb3" + tw, [128, 16, 512], BF16))
        WB.append(P.sbuf("wb4" + tw, [128, 16, 512], BF16))
        for g in range(2):
            wb = load_w(w_glu, 0, 8, g * 512, 512)
            for ft in range(4):
                f = g * 4 + ft
                ps = nps()
                projF(wb, 8, ft * 128, 128, gT, ntok, ps)
                act(B.sgate[:, ft, :ntok], ps[:, :ntok], AF.Sigmoid, [ps, bglu], [(B.sgate, ft)], bias=bglu[:, f:f + 1])
                tt('dve', B.yaT[:, f, :ntok], B.sgate[:, ft, :ntok], gT[:, f, :ntok], ALU.mult, [(B.sgate, ft), gT], [B.yaT])
        for g in range(4):
            wb = load_w(w_in, 0, 16, C_GA + g * 512, 512)
            for ft in range(4):
                ps = nps()
                projF(wb, 16, ft * 128, 128, xT, ntok, ps)
                act(B.sgate[:, ft, :ntok], ps[:, :ntok], AF.Sigmoid, [ps], [(B.sgate, ft)])
            wb = load_w(w_aup, 0, 8, g * 512, 512)
            for ft in range(4):
                ps = nps()
                projF(wb, 8, ft * 128, 128, B.yaT, ntok, ps)
                tt('dve', B.sgate[:, ft, :ntok], ps[:, :ntok], B.sgate[:, ft, :ntok], ALU.mult, [ps, (B.sgate, ft)], [(B.sgate, ft)])
            wb = load_w(w_in, 0, 16, C_GB + g * 512, 512)
            for ft in range(4):
                ps = nps()
                projF(wb, 16, ft * 128, 128, xT, ntok, ps)
                act(B.sgb[:, ft, :ntok], ps[:, :ntok], AF.Sigmoid, [ps], [(B.sgb, ft)])
            wb = load_w(w_bup, 0, 8, g * 512, 512)
            for ft in range(4):
                ps = nps()
                projF(wb, 8, ft * 128, 128, ybT, ntok, ps)
                tt('dve', B.sgb[:, ft, :ntok], ps[:, :ntok], B.sgb[:, ft, :ntok], ALU.mult, [ps, (B.sgb, ft)], [(B.sgb, ft)])
                tt('pool', B.mixT[:, g * 4 + ft, :ntok], B.sgb[:, ft, :ntok], B.sgate[:, ft, :ntok], ALU.add,
                   [(B.sgb, ft), (B.sgate, ft)], [B.mixT])
        for cg in range(4):
            wb = load_w(w_out, 0, 16, cg * 512, 512)
            off = 0
            for i, (L, _, _) in enumerate(tiles):
                ps = nps()
                for k in range(16):
                    mm(ps[:L, :512], B.mixT[:, k, off:off + L], wb[:, k, :512], k == 0, k == 15, [wb, B.mixT], ps)
                stt('dve', xtok[:L, i, cg * 512:(cg + 1) * 512], xtok[:L, i, cg * 512:(cg + 1) * 512], ALPHA, ps[:L, :512],
                    ALU.mult, ALU.add, [(xtok, i), ps], [(xtok, i)])
                off += L
        WB.pop()
        WB.pop()
        close_scope()
        B.gb = P.sbuf("gb_%d" % tagc[0], [128, 2, D], F32)
        load_gb(0)
        for i, (L, _, _) in enumerate(tiles):
            layernorm_tile(xtok, i, L, 0)
        make_xT(xtok, tiles, xT)
        close_scope()
        alloc_C(tb)
        tw = open_scope()
        WB.append(P.sbuf("wb3" + tw, [128, 16, 512], BF16))
        WB.append(P.sbuf("wb4" + tw, [128, 16, 512], BF16))
        parts = [(0, 3), (3, 3), (6, 3), (9, 2)]
        for pi, (g0, ng) in enumerate(parts):
            for gg in range(ng):
                g = g0 + gg
                wbg = load_w(w_gate, 0, 16, g * 512, 512)
                wbu = load_w(w_up, 0, 16, g * 512, 512)
                for ft in range(4):
                    ps = nps()
                    projF(wbg, 16, ft * 128, 128, xT, ntok, ps)
                    act(B.sgate[:, ft, :ntok], ps[:, :ntok], AF.Silu, [ps], [(B.sgate, ft)])
                for ft in range(4):
                    ps = nps()
                    projF(wbu, 16, ft * 128, 128, xT, ntok, ps)
                    tt('dve', B.hT[:, gg * 4 + ft, :ntok], ps[:, :ntok], B.sgate[:, ft, :ntok], ALU.mult,
                       [ps, (B.sgate, ft)], [(B.hT, gg * 4 + ft)])
            kc = ng * 4
            for cg in range(4):
                wb = load_w(w_down, g0 * 512, kc, cg * 512, 512)
                off = 0
                for i, (L, _, _) in enumerate(tiles):
                    ps = nps()
                    for k in range(kc):
                        mm(ps[:L, :512], B.hT[:, k, off:off + L], wb[:, k, :512], k == 0, k == kc - 1, [wb, (B.hT, k)], ps)
                    stt('dve', xtok[:L, i, cg * 512:(cg + 1) * 512], xtok[:L, i, cg * 512:(cg + 1) * 512],
                        ALPHA if pi == 0 else 1.0, ps[:L, :512], ALU.mult, ALU.add, [(xtok, i), ps], [(xtok, i)])
                    off += L
        WB.pop()
        WB.pop()
        close_scope()
        B.gb = P.sbuf("gb_%d" % tagc[0], [128, 2, D], F32)
        load_gb(2)
        for i, (L, _, dst) in enumerate(tiles):
            layernorm_tile(xtok, i, L, 2)
            sdma(dst, xtok[:L, i, :], reads=[(xtok, i)], writes=[], owner=xtok)
        close_scope()

    def run_sample_block():
        L = NSAMP
        tiles = [(L, x_s.t[:, :], y_s.t[:, :])]
        alloc_A(128, sample=True)
        B.qtk = P.sbuf("s_qtk", [16, 1024], BF16)
        B.nst = P.sbuf("s_nst", [16, 1024], F32)
        B.carbS = P.sbuf("s_carbS", [128, 64], F32)
        B.Cbb = P.sbuf("s_Cbb", [128, 2, 256], BF16)
        load_x(tiles)
        make_xT(xtok, tiles, xT)
        in_proj(tiles, L, {'u', 'q', 'qtok', 'k', 'v', 'o'})
        abr, abi = s5t[4], s5t[5]
        tt('dve', abr[:], rdec[:], cosT[:, :, 1], ALU.mult, [rdec, cosT], [abr])
        tt('dve', abi[:], rdec[:], sinT[:, :, 1], ALU.mult, [rdec, sinT], [abi])
        for ct in range(8):
            sdma(B.vre[:L, :, :].rearrange("p a b -> p (a b)"), s_re.t[:, ct * 512:(ct + 1) * 512], writes=[B.vre], owner=B.vre)
            sdma(B.vim[:L, :, :].rearrange("p a b -> p (a b)"), s_im.t[:, ct * 512:(ct + 1) * 512], writes=[B.vim], owner=B.vim)
            p0r, p0i, pbr, pbi = nps(), nps(), nps(), nps()
            for i in range(4):
                t = ct * 4 + i
                P.op('pe', lambda e, i=i: e.transpose(p0r[:, i * 16:(i + 1) * 16], B.vre[:L, i, :], ident[:L, :L]), reads=[B.vre, ident], writes=[p0r])
                P.op('pe', lambda e, i=i: e.transpose(p0i[:, i * 16:(i + 1) * 16], B.vim[:L, i, :], ident[:L, :L]), reads=[B.vim, ident], writes=[p0i])
                mm(pbr[:, i * 16:(i + 1) * 16], BTre[:, t, :], B.uT[:, ct, 0:L], True, True, [BTre, B.uT], pbr)
                mm(pbi[:, i * 16:(i + 1) * 16], BTim[:, t, :], B.uT[:, ct, 0:L], True, True, [BTim, B.uT], pbi)
            for i in range(4):
                t = ct * 4 + i
                sl = slice(i * 16, (i + 1) * 16)
                ts('dve', B.tA[:, i, :L], p0i[:, sl], abi[:, t:t + 1], None, ALU.mult, None, [p0i, abi], [(B.tA, i)])
                stt('dve', B.gre[:, i, :L], p0r[:, sl], abr[:, t:t + 1], B.tA[:, i, :L], ALU.mult, ALU.subtract, [p0r, abr, (B.tA, i)], [(B.gre, i)])
                tt('dve', B.gre[:, i, :L], B.gre[:, i, :L], pbr[:, sl], ALU.add, [(B.gre, i), pbr], [(B.gre, i)])
                ts('dve', B.tB[:, i, :L], p0r[:, sl], abi[:, t:t + 1], None, ALU.mult, None, [p0r, abi], [(B.tB, i)])
                stt('dve', B.gim[:, i, :L], p0i[:, sl], abr[:, t:t + 1], B.tB[:, i, :L], ALU.mult, ALU.add, [p0i, abr, (B.tB, i)], [(B.gim, i)])
                tt('dve', B.gim[:, i, :L], B.gim[:, i, :L], pbi[:, sl], ALU.add, [(B.gim, i), pbi], [(B.gim, i)])
            cp('pool', B.pr[0][:, :, :L], B.gre[:, :, :L], [B.gre], [B.pr[0]])
            cp('pool', B.pr[1][:, :, :L], B.gim[:, :, :L], [B.gim], [B.pr[1]])
            psy = nps()
            n = 0
            for i in range(4):
                t = ct * 4 + i
                for j, CT in enumerate((CTre, CTnim)):
                    mm(psy[:, :L], CT[:, t, :], B.pr[j][:, i, :L], n == 0, n == 7, [CT, B.pr[j]], psy)
                    n += 1
            yv = B.ysk[:, :L]
            stt('dve', yv, B.uT[:, ct, 0:L], dsk[:, ct:ct + 1], psy[:, :L], ALU.mult, ALU.add, [B.uT, dsk, psy], [B.ysk])
            t1 = B.tA[:, 0, :L]
            t2 = B.tB[:, 0, :L]
            tt('pool', t1, yv, yv, ALU.mult, [B.ysk], [B.tA])
            ts('pool', t1, t1, 0.044715, 1.0, ALU.mult, ALU.add, [B.tA], [B.tA])
            tt('pool', t1, t1, yv, ALU.mult, [B.tA, B.ysk], [B.tA])
            act(t2, t1, AF.Sigmoid, [B.tA], [B.tB], scale=1.5957691216057308)
            tt('dve', gT[:, ct, 0:L], yv, t2, ALU.mult, [B.ysk, B.tB], [gT])
            por, poi = nps(), nps()
            for i in range(4):
                P.op('pe', lambda e, i=i: e.transpose(por[:L, i * 128:(i + 1) * 128], B.gre[:, i, :L], ident[:, :]), reads=[B.gre, ident], writes=[por])
                P.op('pe', lambda e, i=i: e.transpose(poi[:L, i * 128:(i + 1) * 128], B.gim[:, i, :L], ident[:, :]), reads=[B.gim, ident], writes=[poi])
            act(B.vre[:L, :, :].rearrange("p a b -> p (a b)"), por[:L, :512], AF.Copy, [por], [B.vre])
            act(B.vim[:L, :, :].rearrange("p a b -> p (a b)"), poi[:L, :512], AF.Copy, [poi], [B.vim])
            sdma(o_sre.t[:, ct * 512:(ct + 1) * 512], B.vre[:L, :, :].rearrange("p a b -> p (a b)"), reads=[B.vre], owner=B.vre)
            sdma(o_sim.t[:, ct * 512:(ct + 1) * 512], B.vim[:L, :, :].rearrange("p a b -> p (a b)"), reads=[B.vim], owner=B.vim)
        m0 = B.negM
        sdma(m0[:, :L], s_mT.t[:, :], writes=[B.negM], owner=B.negM)
        act(B.fpr[:, :L], B.fpr[:, :L], AF.Exp, [B.fpr], [B.fpr], scale=-1.0)
        act(B.fpr[:, :L], B.fpr[:, :L], AF.Ln, [B.fpr], [B.fpr], bias=1.0)
        tt('dve', B.Arow[:, :L], B.igr[:, :L], B.fpr[:, :L], ALU.add, [B.igr, B.fpr], [B.Arow])
        tt('dve', B.Frow[:, :L], m0[:, :L], B.Arow[:, :L], ALU.max, [B.negM, B.Arow], [B.Frow])
        tt('dve', B.R4[:, 0, :L], m0[:, :L], B.Frow[:, :L], ALU.subtract, [B.negM, B.Frow], [(B.R4, 0)])
        tt('dve', B.R4[:, 1, :L], B.Arow[:, :L], B.Frow[:, :L], ALU.subtract, [B.Arow, B.Frow], [(B.R4, 1)])
        tt('dve', B.R4[:, 2, :L], B.fpr[:, :L], B.Frow[:, :L], ALU.subtract, [B.fpr, B.Frow], [(B.R4, 2)])
        for r in range(3):
            act(B.R4[:, r, :L], B.R4[:, r, :L], AF.Exp, [(B.R4, r)], [(B.R4, r)])
        tt('dve', B.R4[:, 3, :L], B.Frow[:, :L], B.fpr[:, :L], ALU.subtract, [B.Frow, B.fpr], [(B.R4, 3)])
        sdma(o_smT.t[:, :], B.R4[:, 3, :L], reads=[(B.R4, 3)], owner=B.R4)
        pst = nps()
        for r in range(3):
            P.op('pe', lambda e, r=r: e.transpose(pst[:L, r * 4:(r + 1) * 4], B.R4[:, r, :L], ident[:4, :4]), reads=[(B.R4, r), ident], writes=[pst])
        cp('dve', colsb[:L, 0:12], pst[:L, :12], [pst], [colsb])
        psc = nps()
        for h in range(H):
            P.op('pe', lambda e, h=h: e.matmul(psc[:, h * 16:(h + 1) * 16], sel[:, h * 128:(h + 1) * 128], B.R4[:, 0, :L], start=True, stop=True),
                 reads=[sel, (B.R4, 0)], writes=[psc])
        cp('dve', B.carbS[:, 0:64], psc[:, 0:64], [psc], [B.carbS])
        sdma(B.nst[:L, 0:1024], s_n.t[:, :], writes=[B.nst], owner=B.nst)
        Cst = [P.sbuf("Cst%d" % i, [128, 2, 256], F32) for i in range(3)]
        Cou = [P.sbuf("Cou%d" % i, [128, 2, 256], F32) for i in range(2)]
        Qm = P.sbuf("Qm", [128, 2, 16, 16], BF16)
        kwb = P.sbuf("kwb", [16, 256], BF16)
        for h in range(H):
            hs = slice(h * 256, (h + 1) * 256)
            qh = B.qtk[:L, hs]
            tt('dve', B.hh[:L, :], qh, B.ktok[:L, 0, hs], ALU.mult, [B.qtk, (B.ktok, 0)], [B.hh])
            act(B.hh[:L, :], B.hh[:L, :], AF.Copy, [B.hh], [B.hh, (sm, 12)], accum_out=sm[:L, 12:13])
            tt('dve', B.hh[:L, :], qh, B.nst[:L, hs], ALU.mult, [B.qtk, B.nst], [B.hh])
            act(B.hh[:L, :], B.hh[:L, :], AF.Copy, [B.hh], [B.hh, (sm, 13)], accum_out=sm[:L, 13:14])
            tt('dve', sm[:L, 14:15], sm[:L, 12:13], colsb[:L, 4 + h:5 + h], ALU.mult, [(sm, 12), colsb], [(sm, 14)])
            stt('dve', B.nd[:L, 256:257], sm[:L, 13:14], colsb[:L, h:h + 1], sm[:L, 14:15], ALU.mult, ALU.add,
                [(sm, 13), colsb, (sm, 14)], [B.nd])
            ts('dve', B.kwt[:L, :], B.ktok[:L, 0, hs], colsb[:L, 4 + h:5 + h], None, ALU.mult, None, [(B.ktok, 0), colsb], [B.kwt])
            stt('dve', B.nst[:L, hs], B.nst[:L, hs], colsb[:L, h:h + 1], B.kwt[:L, :],
                ALU.mult, ALU.add, [B.nst, colsb, B.kwt], [B.nst])
            for c in range(2):
                qa = B.qT[:, 2 * h + c, 0:L]
                qbc = bass.AP(qa.tensor, qa.offset, [list(qa.ap[0]), [0, L], list(qa.ap[1])])
                tt('dve', Qm[:, c, :, :], qbc, emask[:, :].rearrange("p (a b) -> p a b", a=L), ALU.mult, [B.qT, emask], [(Qm, c)])
            psE = PSX
            for b in range(L):
                cs_ = Cst[(h * L + b) % 3]
                co_ = Cou[(h * L + b) % 2]
                P.dma('pool', cs_[:, :, :], s_c.t[b, h].rearrange("(c p) v -> p c v", p=128), writes=[cs_], owner=cs_)
                act(B.Cbb[:, :, :], cs_[:, :, :], AF.Copy, [cs_], [B.Cbb])
                for c in range(2):
                    mm(psE[:L, :256], Qm[:, c, b, :], B.Cbb[:, c, :],
                       (b == 0 and c == 0), (b == L - 1 and c == 1), [(Qm, c), B.Cbb], psE)
                ts('dve', kwb[:L, :], B.kwt[:L, :], ident[:L, b:b + 1], None, ALU.mult, None, [B.kwt, ident], [kwb])
                psU = nps()
                for c in range(2):
                    mm(psU[:, c * 256:(c + 1) * 256], kwb[:L, c * 128:(c + 1) * 128], B.v1[:L, 0, h, 0:256], True, True, [kwb, (B.v1, 0)], psU)
                stt('dve', co_[:, :, :].rearrange("p a b -> p (a b)"), cs_[:, :, :].rearrange("p a b -> p (a b)"),
                    B.carbS[:, h * 16 + b:h * 16 + b + 1], psU[:, :512], ALU.mult, ALU.add, [cs_, B.carbS, psU], [co_])
                sdma(o_sc.t[b, h].rearrange("(c p) v -> p c v", p=128), co_[:, :, :], reads=[co_], owner=co_)
            ts('dve', B.tmpi[:L, 0:256], B.v1[:L, 0, h, 0:256], sm[:L, 14:15], None, ALU.mult, None, [(B.v1, 0), (sm, 14)], [B.tmpi])
            stt('dve', B.nd[:L, 0:256], psE[:L, :256], colsb[:L, h:h + 1], B.tmpi[:L, 0:256], ALU.mult, ALU.add, [psE, colsb, B.tmpi], [B.nd])
            mlstm_out(h, 0, L)
        sdma(o_sn.t[:, :], B.nst[:L, 0:1024], reads=[B.nst], owner=B.nst)
        close_scope()
        tail(tiles, L, 128, gelu_pending=False)

    gfi = [0]

    def run_prompt_block(tiles, full, mask_off=None, fill=()):
        ntok = sum(t[0] for t in tiles)
        tb = max(128, ntok)
        alloc_A(tb, state_only=not full)
        if not full:
            WB.append(P.sbuf("wb3_%d" % tagc[0], [128, 16, 512], BF16))
        load_x(tiles)
        make_xT(xtok, tiles, xT)
        in_proj(tiles, ntok, {'u', 'q', 'kF', 'k', 'v', 'o'} if full else {'u', 'k', 'v'}, mask_off)
        gate_rows(ntok)
        def do_fill(n):
            for _ in range(n):
                if fill and gfi[0] < len(fill):
                    load_w(*fill[gfi[0]])
                    gfi[0] += 1
        per = 1 if fill else 0
        itn = [0]
        off = 0
        for i, (L, _, _) in enumerate(tiles):
            g1 = s5_chunk(off, L, full)
            g2 = mlstm_chunk(i, off, L, full)
            live1, live2 = True, True
            while live1 or live2:
                for _ in range(2):
                    if live1:
                        live1 = next(g1, 'done') != 'done'
                if live2:
                    live2 = next(g2, 'done') != 'done'
                itn[0] += 1
                if full or itn[0] <= 5 * len(tiles) - 1:
                    do_fill(per)
            off += L
        gate_rows_end(ntok)
        if not full:
            WB.pop()
        close_scope()
        if full:
            tail(tiles, ntok, tb)
        return tiles[-1][0]

    npre_full = (NPRE - NMETA) // TB
    fills = []
    for c0 in (C_Q, C_O, C_GA, C_GA + 1024, C_GB, C_GB + 1024):
        fills += [(w_in, 0, 16, c0, 512), (w_in, 0, 16, c0 + 512, 512)]
    fills += [(w_glu, 0, 8, g * 512, 512) for g in range(2)]
    fills += [(w_aup, 0, 8, g * 512, 512) for g in range(4)] + [(w_bup, 0, 8, g * 512, 512) for g in range(4)]
    fills += [(w_out, 0, 16, g * 512, 512) for g in range(4)]
    for (g0, ng) in ((0, 3), (3, 3), (6, 3), (9, 2)):
        for gg in range(ng):
            fills += [(w_gate, 0, 16, (g0 + gg) * 512, 512), (w_up, 0, 16, (g0 + gg) * 512, 512)]
        fills += [(w_down, g0 * 512, ng * 4, cg * 512, 512) for cg in range(4)]
    for b in range(npre_full):
        tiles = [(128, x_pre.t[b * TB + i * 128: b * TB + (i + 1) * 128, :], None) for i in range(NCH)]
        run_prompt_block(tiles, False, b * TB, fills)
    run_prompt_block([(NMETA, x_pre.t[NPRE - NMETA:NPRE, :], None)], False, NPRE - NMETA)
    if debug == 'lead':
        dump(hst, hst[:, 0, :], 32)
        dump(Cn, Cn[:, 0, 0, :], 257, c0=64)
        P.finish('sp')
        return
    lastL = NMETA
    nblocks = NB if debug is None else 1
    for b in range(nblocks):
        tiles = [(128, x_p.t[b * TB + i * 128: b * TB + (i + 1) * 128, :], y_p.t[b * TB + i * 128: b * TB + (i + 1) * 128, :])
                 for i in range(NCH)]
        lastL = run_prompt_block(tiles, True)

    s5_state_out(lastL, o_pre, o_pim)
    for h in range(H):
        for c in range(2):
            sdma(o_pc.t[h, c * 128:(c + 1) * 128, :], Cn[:, h, c, 0:256], reads=[(Cn, h)], writes=[], owner=Cn)
    with nc.allow_non_contiguous_dma(reason="small state column"):
        sdma(o_pn.t[:, :], Cn[:, :, :, 256], reads=[Cn], writes=[], owner=Cn)
    tt('dve', carr[:, 1:2], Fc[:, 0:1], Mext[:, 0:1], ALU.add, [Fc, Mext], [carr])
    sdma(o_pm.t[:, :], carr[:, 1:2], reads=[carr], writes=[], owner=carr)
    if debug is None or debug == 'samp':
        run_sample_block()

    P.finish('sp')


def _consts():
    c = {}
    c["c_id"] = np.eye(128, dtype=np.float32)
    s = np.arange(128)
    c["c_nm"] = np.where(s[:, None] <= s[None, :], 0.0, NEG).astype(np.float32)
    sel = np.zeros((4, 4, 128), np.float32)
    for h in range(4):
        sel[h, h, :] = 1.0
    c["c_sel"] = sel.reshape(4, 512)
    c["c_j"] = np.broadcast_to(np.arange(130, dtype=np.float32), (128, 130)).copy()
    em = np.zeros((128, NSAMP, NSAMP), np.float32)
    for b in range(NSAMP):
        em[:, b, b] = 1.0
    c["c_em"] = em.reshape(128, NSAMP * NSAMP)
    return c


def _lay_gp(a):
    return np.ascontiguousarray(a.reshape(32, 128).T)


def _pad_b(b):
    out = np.zeros((128, 32, 128), np.float32)
    for g in range(G):
        t, g2 = g // 2, g % 2
        c0 = 32 * (t % 4) + 16 * g2
        out[g2 * 64:(g2 + 1) * 64, t, c0:c0 + 16] = b[g]
    return out


def _pad_c(c):
    out = np.zeros((128, 32, 128), np.float32)
    for g in range(G):
        t, g2 = g // 2, g % 2
        r0 = 32 * (t % 4) + 16 * g2
        out[r0:r0 + 16, t, g2 * 64:(g2 + 1) * 64] = c[g]
    return out


def _chan(v):
    return np.ascontiguousarray(v.reshape(8, 128).T)


_NC_CACHE = {}


def kernel(x_prompt, x_sample, state_ssm_re, state_ssm_im, state_mlstm_c, state_mlstm_n, state_mlstm_m,
           meta_tokens, w_in, b_if, ssm_a_re, ssm_a_im, ssm_log_dt, ssm_b_re, ssm_b_im, ssm_c_re, ssm_c_im,
           ssm_d, w_glu, b_glu, w_a_up, mh_gain, w_b_up, w_out, ln1_g, ln1_b, w_gate, w_up, w_down,
           ln2_g, ln2_b, _debug=None):
    f = lambda a: np.ascontiguousarray(np.asarray(a, dtype=np.float32))
    shared = {
        "w_in": f(w_in[0]),
        "b_if": f(np.asarray(b_if[0]).reshape(2, 4).T),
        "a_re": _lay_gp(f(ssm_a_re[0])), "a_im": _lay_gp(f(ssm_a_im[0])), "l_dt": _lay_gp(f(ssm_log_dt[0])),
        "bp_re": _pad_b(f(ssm_b_re[0])), "bp_im": _pad_b(f(ssm_b_im[0])),
        "cp_re": _pad_c(f(ssm_c_re[0])), "cp_im": _pad_c(f(ssm_c_im[0])),
        "d_sk": _chan(f(ssm_d[0])), "w_glu": f(w_glu[0]), "b_glu": _chan(f(b_glu[0])),
        "w_aup": f(w_a_up[0]), "mhg": _chan(f(mh_gain[0])), "w_bup": f(w_b_up[0]), "w_out": f(w_out[0]),
        "ln_gb": f(np.stack([ln1_g[0], ln1_b[0], ln2_g[0], ln2_b[0]])),
        "w_gate": f(w_gate[0]), "w_up": f(w_up[0]), "w_down": f(w_down[0]),
    }
    shared.update(_consts())
    xp = f(x_prompt)
    meta_f = f(meta_tokens)
    xs = f(x_sample).reshape(128, D)
    sre = f(state_ssm_re[0]).reshape(128, G * PST)
    sim = f(state_ssm_im[0]).reshape(128, G * PST)
    sc = f(state_mlstm_c[0])
    sn = f(state_mlstm_n[0]).reshape(128, H * DK)
    smm = f(state_mlstm_m[0])
    in_maps = []
    for c in range(8):
        m = dict(shared)
        sq, half = c // 2, c % 2
        m["x_p"] = np.ascontiguousarray(xp[sq, half * NMAIN:(half + 1) * NMAIN])
        gm = np.zeros((2, 4, NPRE), np.float32)
        if half == 0:
            m["x_pre"] = np.ascontiguousarray(np.concatenate([np.zeros((NMAIN, D), np.float32), meta_f]))
            gm[0, :, :NMAIN] = NEG
            gm[1, :, :NMAIN] = -NEG
        else:
            m["x_pre"] = np.ascontiguousarray(np.concatenate([meta_f, xp[sq, :NMAIN]]))
        m["g_mk"] = gm
        sl = slice(c * NSAMP, (c + 1) * NSAMP)
        m["x_s"] = xs[sl]
        m["s_re"] = sre[sl]
        m["s_im"] = sim[sl]
        m["s_c"] = sc[sl]
        m["s_n"] = sn[sl]
        m["s_mT"] = np.ascontiguousarray(smm[sl].T)
        in_maps.append(m)
    key = _debug
    if key not in _NC_CACHE:
        _NC_CACHE[key] = build(_debug)
    nc = _NC_CACHE[key]
    res = run_bass_kernel_spmd(nc, in_maps, core_ids=list(range(8)))
    R = res.results
    if _debug:
        return R
    y_prompt = np.stack([np.concatenate([R[2 * c]["y_p"], R[2 * c + 1]["y_p"]]) for c in range(4)])
    y_sample = np.concatenate([R[c]["y_s"] for c in range(8)]).reshape(128, 1, D)

    def gp(a):
        return np.ascontiguousarray(a.T).reshape(G, PST)
    p_re = np.stack([gp(R[2 * c + 1]["o_pre"]) for c in range(4)])[None]
    p_im = np.stack([gp(R[2 * c + 1]["o_pim"]) for c in range(4)])[None]
    p_c = np.stack([R[2 * c + 1]["o_pc"] for c in range(4)])[None]
    p_n = np.stack([R[2 * c + 1]["o_pn"].reshape(128, H, 2).transpose(1, 2, 0).reshape(H, DK) for c in range(4)])[None]
    p_m = np.stack([R[2 * c + 1]["o_pm"].reshape(H) for c in range(4)])[None]
    s_re_o = np.concatenate([R[c]["o_sre"] for c in range(8)]).reshape(1, 128, G, PST)
    s_im_o = np.concatenate([R[c]["o_sim"] for c in range(8)]).reshape(1, 128, G, PST)
    s_c_o = np.concatenate([R[c]["o_sc"] for c in range(8)])[None]
    s_n_o = np.concatenate([R[c]["o_sn"] for c in range(8)]).reshape(1, 128, H, DK)
    s_m_o = np.concatenate([R[c]["o_smT"].T for c in range(8)])[None]
    return (y_prompt, y_sample, p_re, p_im, p_c, p_n, p_m, s_re_o, s_im_o, s_c_o, s_n_o, s_m_o)
```
